# Optimizing a Trainium2 kernel written in Bass

```python
import math
import jax, jax.numpy as jnp
from jax import lax
import numpy as np

D_MODEL = 1024
BATCH = 4
SEQ = 8192
DEPTH = 2

N_HEADS_ATTN = 8
HEAD_DIM_ATTN = 64
V_DIM_ATTN = 2 * HEAD_DIM_ATTN
ATTN_WIDTH = N_HEADS_ATTN * V_DIM_ATTN
Q_BLOCK = 128
SSM_WIDTH = D_MODEL
SSM_HEADDIM = 64
N_HEADS_SSM = SSM_WIDTH // SSM_HEADDIM
SSM_GROUPS = 4
SSM_HEADS_PER_GROUP = N_HEADS_SSM // SSM_GROUPS
SSM_STATE = 128
CONV_WIDTH = 5
CONV_CH = SSM_WIDTH + 2 * SSM_GROUPS * SSM_STATE
CHUNK = 128
D_MIX = ATTN_WIDTH + SSM_WIDTH
Q_COLS = N_HEADS_ATTN * 2 * HEAD_DIM_ATTN
K_COLS = N_HEADS_ATTN * 2 * HEAD_DIM_ATTN
V_COLS = ATTN_WIDTH
Z_COLS = SSM_WIDTH
XBC_COLS = CONV_CH
DT_COLS = N_HEADS_SSM
IN_COLS = Q_COLS + K_COLS + V_COLS + Z_COLS + XBC_COLS + DT_COLS
D_FF = -(-8 * D_MODEL // (3 * 256)) * 256
DEEPNORM_ALPHA = (2 * DEPTH) ** 0.25
DEEPNORM_BETA = (8 * DEPTH) ** -0.25
LN_EPS = 1e-5
RMS_EPS = 1e-5

kernel_name = "hymba_diffattn_ssd_deepnorm_encoder"


def layernorm(x, g, b):
    x32 = x.astype(jnp.float32)
    mu = jnp.mean(x32, axis=-1, keepdims=True)
    var = jnp.mean(jnp.square(x32 - mu), axis=-1, keepdims=True)
    y = (x32 - mu) * lax.rsqrt(var + LN_EPS)
    return (y * g.astype(jnp.float32) + b.astype(jnp.float32)).astype(x.dtype)


def rmsnorm(x, w):
    x32 = x.astype(jnp.float32)
    y = x32 * lax.rsqrt(jnp.mean(jnp.square(x32), axis=-1, keepdims=True) + RMS_EPS)
    return (y * w.astype(jnp.float32)).astype(x.dtype)


def alibi_slopes(n_heads):
    return jnp.exp2(-8.0 * jnp.arange(1, n_heads + 1, dtype=jnp.float32) / n_heads)


def diff_attention(q, k, v, lq1, lk1, lq2, lk2, subln_w, lambda_init):
    b, S = q.shape[0], q.shape[1]
    nblk = S // Q_BLOCK
    f32 = jnp.float32
    lam = (jnp.exp(jnp.sum(lq1.astype(f32) * lk1.astype(f32)))
           - jnp.exp(jnp.sum(lq2.astype(f32) * lk2.astype(f32))) + lambda_init)
    slopes = alibi_slopes(N_HEADS_ATTN)
    k_t = k.transpose(0, 2, 3, 1, 4)
    v_t = v.transpose(0, 2, 1, 3)
    q_blocks = q.reshape(b, nblk, Q_BLOCK, N_HEADS_ATTN, 2, HEAD_DIM_ATTN)
    q_blocks = q_blocks.transpose(1, 0, 3, 4, 2, 5)
    pos_k = jnp.arange(S, dtype=jnp.int32)
    starts = jnp.arange(nblk, dtype=jnp.int32) * Q_BLOCK
    scale = HEAD_DIM_ATTN ** -0.5

    def block(args):
        qb, start = args
        s = jnp.einsum('bhcqd,bhckd->bhcqk', qb, k_t).astype(f32) * scale
        pos_q = start + jnp.arange(Q_BLOCK, dtype=jnp.int32)
        dist = jnp.abs(pos_q[:, None] - pos_k[None, :]).astype(f32)
        s = s - slopes[:, None, None, None] * dist
        p = jax.nn.softmax(s, axis=-1)
        a = p[:, :, 0] - lam * p[:, :, 1]
        return jnp.einsum('bhqk,bhkd->bhqd', a.astype(v_t.dtype), v_t)

    o = lax.map(block, (q_blocks, starts))
    o = o.transpose(1, 0, 3, 2, 4).reshape(b, S, N_HEADS_ATTN, V_DIM_ATTN)
    o = rmsnorm(o, subln_w) * (1.0 - lambda_init)
    return o.reshape(b, S, ATTN_WIDTH)


def depthwise_conv_centred(u, w, bias):
    out = lax.conv_general_dilated(
        u, w[:, None, :].astype(u.dtype), window_strides=(1,),
        padding=[(CONV_WIDTH // 2, CONV_WIDTH // 2)],
        dimension_numbers=('NWC', 'WIO', 'NWC'),
        feature_group_count=u.shape[-1])
    return out + bias.astype(out.dtype)


def segsum_exp(cs):
    T = cs.shape[-1]
    diff = cs[..., :, None] - cs[..., None, :]
    mask = jnp.tril(jnp.ones((T, T), dtype=bool))
    return jnp.exp(jnp.where(mask, diff, -jnp.inf))


def ssd_scan(xh, dt, A, Bg, Cg):
    b, S, G, R, P = xh.shape
    nc = S // CHUNK
    dA = dt * A
    xc = (xh * dt[..., None].astype(xh.dtype)).reshape(b, nc, CHUNK, G, R, P)
    Bc = Bg.reshape(b, nc, CHUNK, G, SSM_STATE)
    Cc = Cg.reshape(b, nc, CHUNK, G, SSM_STATE)
    acs = jnp.cumsum(dA.reshape(b, nc, CHUNK, G, R).transpose(0, 3, 4, 1, 2), axis=-1)
    L = segsum_exp(acs)
    cb = jnp.einsum('bclgn,bcsgn->bcgls', Cc, Bc)
    y_diag = jnp.einsum('bcgls,bgrcls,bcsgrp->bclgrp', cb, L, xc)
    decay_states = jnp.exp(acs[..., -1:] - acs)
    states = jnp.einsum('bclgn,bgrcl,bclgrp->bcgrpn', Bc, decay_states, xc)
    states = jnp.concatenate([jnp.zeros_like(states[:, :1]), states], axis=1)
    chunk_tot = jnp.pad(acs[..., -1], ((0, 0), (0, 0), (0, 0), (1, 0)))
    decay_chunk = segsum_exp(jnp.cumsum(chunk_tot, axis=-1))
    new_states = jnp.einsum('bgrzc,bcgrpn->bzgrpn', decay_chunk, states)
    states = new_states[:, :-1]
    y_off = jnp.einsum('bclgn,bcgrpn,bgrcl->bclgrp', Cc, states, jnp.exp(acs))
    return (y_diag + y_off).reshape(b, S, G, R, P)


def ssd_mixer(z, xbc, dt_raw, conv_w, conv_b, dt_bias_fwd, dt_bias_bwd,
              a_log_fwd, a_log_bwd, d_skip, norm_w):
    b, S = z.shape[0], z.shape[1]
    xbc = jax.nn.silu(depthwise_conv_centred(xbc, conv_w, conv_b))
    xs = xbc[..., :SSM_WIDTH]
    Bg = xbc[..., SSM_WIDTH:SSM_WIDTH + SSM_GROUPS * SSM_STATE].reshape(b, S, SSM_GROUPS, SSM_STATE)
    Cg = xbc[..., SSM_WIDTH + SSM_GROUPS * SSM_STATE:].reshape(b, S, SSM_GROUPS, SSM_STATE)
    xh = xs.reshape(b, S, SSM_GROUPS, SSM_HEADS_PER_GROUP, SSM_HEADDIM)
    dt32 = dt_raw.astype(jnp.float32)

    def direction(xh_d, B_d, C_d, dt_d, dt_bias, a_log):
        dt = jax.nn.softplus(dt_d + dt_bias.astype(jnp.float32))
        dt = dt.reshape(b, S, SSM_GROUPS, SSM_HEADS_PER_GROUP)
        A = -jnp.exp(a_log.astype(jnp.float32)).reshape(SSM_GROUPS, SSM_HEADS_PER_GROUP)
        return ssd_scan(xh_d, dt, A, B_d, C_d)

    flip = lambda t: jnp.flip(t, axis=1)
    y_f = direction(xh, Bg, Cg, dt32, dt_bias_fwd, a_log_fwd)
    y_b = flip(direction(flip(xh), flip(Bg), flip(Cg), flip(dt32), dt_bias_bwd, a_log_bwd))
    D = d_skip.reshape(SSM_GROUPS, SSM_HEADS_PER_GROUP)[..., None]
    y = (y_f + y_b + D * xh).astype(z.dtype).reshape(b, S, SSM_WIDTH)
    g = (y * jax.nn.silu(z)).reshape(b, S, SSM_GROUPS, SSM_WIDTH // SSM_GROUPS)
    g32 = g.astype(jnp.float32)
    g32 = g32 * lax.rsqrt(jnp.mean(jnp.square(g32), axis=-1, keepdims=True) + RMS_EPS)
    return (g32.reshape(b, S, SSM_WIDTH) * norm_w.astype(jnp.float32)).astype(z.dtype)


def setup_inputs(seed: int = 0) -> dict:
    key = jax.random.key(seed)
    ks = jax.random.split(key, 24)
    f32 = jnp.float32

    def nrm(k, shape, scale):
        return jax.random.normal(k, shape, f32) * scale

    def dt_bias(k):
        dt = jnp.exp(jax.random.uniform(k, (DEPTH, N_HEADS_SSM), f32, math.log(1e-3), math.log(1e-1)))
        return dt + jnp.log(-jnp.expm1(-dt))

    def a_log(k):
        return jnp.log(jax.random.uniform(k, (DEPTH, N_HEADS_SSM), f32, 1.0, 16.0))

    return {
        "x": nrm(ks[0], (BATCH, SEQ, D_MODEL), 1.0),
        "w_in": nrm(ks[1], (DEPTH, D_MODEL, IN_COLS), D_MODEL ** -0.5),
        "lambda_q1": nrm(ks[2], (DEPTH, HEAD_DIM_ATTN), 0.1),
        "lambda_k1": nrm(ks[3], (DEPTH, HEAD_DIM_ATTN), 0.1),
        "lambda_q2": nrm(ks[4], (DEPTH, HEAD_DIM_ATTN), 0.1),
        "lambda_k2": nrm(ks[5], (DEPTH, HEAD_DIM_ATTN), 0.1),
        "subln_w": 1.0 + nrm(ks[6], (DEPTH, V_DIM_ATTN), 0.02),
        "conv_w": nrm(ks[7], (DEPTH, CONV_WIDTH, CONV_CH), CONV_WIDTH ** -0.5),
        "conv_b": nrm(ks[8], (DEPTH, CONV_CH), 0.02),
        "dt_bias_fwd": dt_bias(ks[9]),
        "dt_bias_bwd": dt_bias(ks[10]),
        "a_log_fwd": a_log(ks[11]),
        "a_log_bwd": a_log(ks[12]),
        "d_skip": 1.0 + nrm(ks[13], (DEPTH, N_HEADS_SSM), 0.02),
        "ssm_norm_w": 1.0 + nrm(ks[14], (DEPTH, SSM_WIDTH), 0.02),
        "w_out": nrm(ks[15], (DEPTH, D_MIX, D_MODEL), D_MIX ** -0.5 * DEEPNORM_BETA),
        "ln1_g": 1.0 + nrm(ks[16], (DEPTH, D_MODEL), 0.02),
        "ln1_b": nrm(ks[17], (DEPTH, D_MODEL), 0.02),
        "w_gate": nrm(ks[18], (DEPTH, D_MODEL, D_FF), D_MODEL ** -0.5),
        "w_up": nrm(ks[19], (DEPTH, D_MODEL, D_FF), D_MODEL ** -0.5),
        "w_down": nrm(ks[20], (DEPTH, D_FF, D_MODEL), D_FF ** -0.5 * DEEPNORM_BETA),
        "ln2_g": 1.0 + nrm(ks[21], (DEPTH, D_MODEL), 0.02),
        "ln2_b": nrm(ks[22], (DEPTH, D_MODEL), 0.02),
    }


def reference(x, w_in, lambda_q1, lambda_k1, lambda_q2, lambda_k2, subln_w, conv_w, conv_b,
              dt_bias_fwd, dt_bias_bwd, a_log_fwd, a_log_bwd, d_skip, ssm_norm_w, w_out,
              ln1_g, ln1_b, w_gate, w_up, w_down, ln2_g, ln2_b):
    b, S = x.shape[0], x.shape[1]
    splits = np.cumsum([Q_COLS, K_COLS, V_COLS, Z_COLS, XBC_COLS]).tolist()
    for l in range(DEPTH):
        lambda_init = 0.8 - 0.6 * math.exp(-0.3 * l)
        h = jnp.einsum('bsd,de->bse', x, w_in[l])
        q, k, v, z, xbc, dt_raw = jnp.split(h, splits, axis=-1)
        attn_out = diff_attention(
            q.reshape(b, S, N_HEADS_ATTN, 2, HEAD_DIM_ATTN),
            k.reshape(b, S, N_HEADS_ATTN, 2, HEAD_DIM_ATTN),
            v.reshape(b, S, N_HEADS_ATTN, V_DIM_ATTN),
            lambda_q1[l], lambda_k1[l], lambda_q2[l], lambda_k2[l], subln_w[l], lambda_init)
        ssd_out = ssd_mixer(z, xbc, dt_raw, conv_w[l], conv_b[l], dt_bias_fwd[l], dt_bias_bwd[l],
                            a_log_fwd[l], a_log_bwd[l], d_skip[l], ssm_norm_w[l])
        mix = jnp.einsum('bse,ed->bsd', jnp.concatenate([attn_out, ssd_out], axis=-1), w_out[l])
        x = layernorm(DEEPNORM_ALPHA * x + mix, ln1_g[l], ln1_b[l])
        hid = jax.nn.silu(jnp.einsum('bsd,df->bsf', x, w_gate[l])) * jnp.einsum('bsd,df->bsf', x, w_up[l])
        ffn = jnp.einsum('bsf,fd->bsd', hid, w_down[l])
        x = layernorm(DEEPNORM_ALPHA * x + ffn, ln2_g[l], ln2_b[l])
    return x
```

```python
import bisect
from contextlib import ExitStack
import numpy as np
import concourse.bass as bass
import concourse.mybir as mybir

F32 = mybir.dt.float32
BF16 = mybir.dt.bfloat16
AF = mybir.ActivationFunctionType
ALU = mybir.AluOpType
AX = mybir.AxisListType

D_MODEL = 1024
D_FF = 2816
LN_EPS = 1e-5
RMS_EPS = 1e-5
ALPHA = 4.0 ** 0.25


class T:
    __slots__ = ("name", "w", "r", "rd")

    def __init__(self, name=""):
        self.name = name
        self.w = None
        self.r = {}
        self.rd = []


class Sched:
    CE = ("pe", "act", "dve", "pool")

    def __init__(self, nc, ndma=16):
        self.nc = nc
        self.eng = {"pe": nc.tensor, "act": nc.scalar, "dve": nc.vector, "pool": nc.gpsimd, "sp": nc.sync}
        self.sem = {e: nc.alloc_semaphore("se_" + e) for e in self.CE}
        self.cnt = {e: 0 for e in self.CE}
        self.ins = {e: [] for e in self.CE}
        self.sig_idx = {e: [] for e in self.CE}
        self.sig_val = {e: [] for e in self.CE}
        self.waited = {e: {} for e in self.eng}
        self.ndma = ndma
        self.dsem = {q: [nc.alloc_semaphore(f"sd_{q}{i}") for i in range(ndma)] for q in ("sp", "pool", "act")}
        self.dn = {q: 0 for q in self.dsem}
        self.keep = [self.sem[e] for e in self.CE]

    def _wait_sem(self, eng, sem, val):
        k = id(sem)
        if self.waited[eng].get(k, 0) >= val:
            return
        self.waited[eng][k] = val
        self.eng[eng].wait_ge(sem, val)

    def _wait(self, eng, ev):
        if ev is None:
            return
        if ev[0] == "c":
            _, e2, idx = ev
            if e2 == "pe" and eng == "pe":
                return
            si = self.sig_idx[e2]
            p = bisect.bisect_left(si, idx)
            if p < len(si):
                val = self.sig_val[e2][p]
            else:
                self.cnt[e2] += 1
                val = self.cnt[e2]
                self.ins[e2][idx].then_inc(self.sem[e2], 1)
                si.append(idx)
                self.sig_val[e2].append(val)
            self._wait_sem(eng, self.sem[e2], val)
        else:
            _, sem, val = ev
            self._wait_sem(eng, sem, val)

    def _deps(self, eng, r, w):
        for t in r:
            self._wait(eng, t.w)
        for t in w:
            self._wait(eng, t.w)
            for ev in t.r.values():
                self._wait(eng, ev)
            for ev in t.rd:
                self._wait(eng, ev)

    def _mark(self, ev, r, w, is_dma):
        for t in r:
            if is_dma:
                t.rd.append(ev)
            else:
                t.r[ev[1]] = ev
        for t in w:
            t.w = ev
            t.r = {}
            t.rd = []

    def c(self, eng, fn, r=(), w=()):
        self._deps(eng, r, w)
        ins = fn(self.eng[eng])
        idx = len(self.ins[eng])
        self.ins[eng].append(ins)
        self._mark(("c", eng, idx), r, w, False)
        return ins

    def dma(self, q, out, in_, r=(), w=()):
        self._deps(q, r, w)
        n = self.dn[q]
        self.dn[q] += 1
        sem = self.dsem[q][n % self.ndma]
        use = n // self.ndma
        if use > 0:
            self._wait_sem(q, sem, 16 * use)
        self.eng[q].dma_start(out=out, in_=in_).then_inc(sem, 16)
        ev = ("d", sem, 16 * (use + 1))
        self._mark(ev, r, w, True)
        return ev

    def barrier(self):
        evs = []
        for e in self.CE:
            if self.ins[e]:
                evs.append(("c", e, len(self.ins[e]) - 1))
        for q in self.dsem:
            n = self.dn[q]
            for i in range(min(n, self.ndma)):
                uses = (n - 1 - i) // self.ndma + 1
                evs.append(("d", self.dsem[q][i], 16 * uses))
        for eng in ("pe", "act", "dve", "pool", "sp"):
            for ev in evs:
                if ev[0] == "c" and ev[1] == eng:
                    continue
                if ev[0] == "c" and ev[1] == "pe" and eng == "pe":
                    continue
                self._wait(eng, ev)
        self.nphase = getattr(self, "nphase", 0) + 1
        for e in self.CE:
            self.sem[e] = self.nc.alloc_semaphore(f"se_{e}_{self.nphase}")
            self.keep.append(self.sem[e])
            self.cnt[e] = 0
            self.ins[e] = []
            self.sig_idx[e] = []
            self.sig_val[e] = []

    def collective(self, kind, pairs, groups):
        if not hasattr(self, "ccsem"):
            self.ccsem = self.nc.alloc_semaphore("cc_sem")
            self.ccn = 0
        for in_ap, out_ap in pairs:
            self.ccn += 1
            self.nc.gpsimd.collective_compute(kind, ALU.bypass, replica_groups=groups, ins=[in_ap], outs=[out_ap]).then_inc(self.ccsem, 1)
        for eng in ("pe", "act", "dve", "pool", "sp"):
            self._wait_sem(eng, self.ccsem, self.ccn)


def _rot(lst, i):
    return lst[i % len(lst)]


class Ctx:
    def __init__(self, nc, es, pfx):
        self.nc, self.es, self.pfx, self.n = nc, es, pfx, 0

    def sb(self, shape, dt, name=None):
        self.n += 1
        h = self.es.enter_context(self.nc.sbuf_tensor(f"{self.pfx}_{name or 's'}{self.n}", list(shape), dt))
        return h

    def ps(self, shape, dt=F32, name=None):
        self.n += 1
        h = self.es.enter_context(self.nc.psum_tensor(f"{self.pfx}_{name or 'p'}{self.n}", list(shape), dt))
        return h


NFM = 16
NCOL = 3080


def phase_inproj(nc, sc, S, x_d, w_d, ident_d, fm_d, v_d, z_d, dt_d, pfx="p1"):
    with ExitStack() as es:
        cx = Ctx(nc, es, pfx)
        Wb = cx.sb([128, 8, NCOL], BF16, "Wb")
        tW = T("Wb")
        ident = cx.sb([128, 128], BF16, "ident")
        tI = T("ident")
        sc.dma("pool", ident[:], ident_d[:, :], w=[tI])
        wv = w_d.rearrange("(c p) n -> p c n", p=128)
        for c in range(8):
            sc.dma("pool", Wb[:, c, :], wv[:, c, :], w=[tW])
        NB = S // 512
        xb = [cx.sb([128, 4, 1024], BF16, "xb") for _ in range(2)]
        txb = [T() for _ in range(2)]
        xT = [cx.sb([128, 8, 512], BF16, "xT") for _ in range(2)]
        txT = [T() for _ in range(2)]
        ptr = [cx.ps([128, 8, 128], BF16, "ptr") for _ in range(2)]
        tptr = [T() for _ in range(2)]
        pp = [cx.ps([128, 512], F32, "pp") for _ in range(4)]
        tpp = [T() for _ in range(4)]
        ob = [cx.sb([128, 512], BF16, "ob") for _ in range(4)]
        tob = [T() for _ in range(4)]
        odt = [cx.sb([128, 8], F32, "odt") for _ in range(2)]
        todt = [T() for _ in range(2)]
        ntr = 0
        npp = 0
        nob = 0
        for j in range(NB):
            xbj, txbj = xb[j % 2], txb[j % 2]
            xTj, txTj = xT[j % 2], txT[j % 2]
            xsrc = x_d(j) if callable(x_d) else x_d[j * 512:(j + 1) * 512, :]
            sc.dma("pool", xbj[:], xsrc.rearrange("(s p) d -> p s d", p=128), w=[txbj])
            for s in range(4):
                p_, tp_ = ptr[ntr % 2], tptr[ntr % 2]
                for c in range(8):
                    sc.c("pe", lambda e, p_=p_, c=c, s=s: e.transpose(p_[:, c, :], xbj[:, s, c * 128:(c + 1) * 128], ident[:]),
                         r=[txbj, tI], w=[tp_])
                dst = xTj[:, :, s * 128:(s + 1) * 128]
                if ntr % 2 == 0:
                    sc.c("act", lambda e, p_=p_, dst=dst: e.activation(dst, p_[:], AF.Copy), r=[tp_], w=[txTj])
                else:
                    sc.c("dve", lambda e, p_=p_, dst=dst: e.tensor_copy(dst, p_[:]), r=[tp_], w=[txTj])
                ntr += 1
            for cc in range(NFM):
                p_, tp_ = pp[npp % 4], tpp[npp % 4]
                npp += 1
                for c in range(8):
                    sc.c("pe", lambda e, p_=p_, c=c, cc=cc: e.matmul(p_[:], Wb[:, c, cc * 128:(cc + 1) * 128], xTj[:, c, :],
                                                                    start=(c == 0), stop=(c == 7)),
                         r=[tW, txTj], w=[tp_])
                o_, to_ = ob[nob % 4], tob[nob % 4]
                scale = 0.125 if cc < 4 else 1.0
                if nob % 2 == 0:
                    sc.c("act", lambda e, p_=p_, o_=o_, scale=scale: e.activation(o_[:], p_[:], AF.Copy, scale=scale),
                         r=[tp_], w=[to_])
                else:
                    sc.c("dve", lambda e, p_=p_, o_=o_, scale=scale: e.tensor_scalar(o_[:], p_[:], scale, None, ALU.mult),
                         r=[tp_], w=[to_])
                nob += 1
                sc.dma("sp", fm_d[cc, :, j * 512:(j + 1) * 512], o_[:], r=[to_])
            for s in range(4):
                for g, (c0, dst) in enumerate(((2048, v_d), (2560, z_d))):
                    p_, tp_ = pp[npp % 4], tpp[npp % 4]
                    npp += 1
                    for c in range(8):
                        sc.c("pe", lambda e, p_=p_, c=c, s=s, c0=c0: e.matmul(p_[:], xTj[:, c, s * 128:(s + 1) * 128],
                                                                            Wb[:, c, c0:c0 + 512], start=(c == 0), stop=(c == 7)),
                             r=[tW, txTj], w=[tp_])
                    o_, to_ = ob[nob % 4], tob[nob % 4]
                    if nob % 2 == 0:
                        sc.c("act", lambda e, p_=p_, o_=o_: e.activation(o_[:], p_[:], AF.Copy), r=[tp_], w=[to_])
                    else:
                        sc.c("dve", lambda e, p_=p_, o_=o_: e.tensor_copy(o_[:], p_[:]), r=[tp_], w=[to_])
                    nob += 1
                    t0 = j * 512 + s * 128
                    sc.dma("sp", dst[t0:t0 + 128, :], o_[:], r=[to_])
                p_, tp_ = pp[npp % 4], tpp[npp % 4]
                npp += 1
                for c in range(8):
                    sc.c("pe", lambda e, p_=p_, c=c, s=s: e.matmul(p_[:, 0:8], xTj[:, c, s * 128:(s + 1) * 128],
                                                                 Wb[:, c, 3072:3080], start=(c == 0), stop=(c == 7)),
                         r=[tW, txTj], w=[tp_])
                o_, to_ = odt[s % 2], todt[s % 2]
                sc.c("dve", lambda e, p_=p_, o_=o_: e.tensor_copy(o_[:], p_[:, 0:8]), r=[tp_], w=[to_])
                t0 = j * 512 + s * 128
                sc.dma("sp", dt_d[t0:t0 + 128, :], o_[:], r=[to_])
        sc.barrier()


def phase_attn(nc, sc, S, fm_d, v_d, ktab_d, qtab_d, dtab_d, ident_d, lam_d, subln_d, lin_d, mix_d, pfx="p2", win_slopes=None, win_th=64.0):
    NKB = S // 128
    NQB = S // 512
    with ExitStack() as es:
        cx = Ctx(nc, es, pfx)
        ident = cx.sb([128, 128], BF16, "ident")
        tI = T()
        sc.dma("pool", ident[:], ident_d[:, :], w=[tI])
        ones = cx.sb([128, 128], BF16, "ones")
        onesf = cx.sb([128, 128], F32, "onesf")
        tC = T()
        sc.c("dve", lambda e: e.memset(ones[:], 1.0), w=[tC])
        sc.c("dve", lambda e: e.memset(onesf[:], 1.0), w=[tC])
        lamt = cx.sb([128, 4, 64], F32, "lamt")
        tl = T()
        for i in range(4):
            sc.dma("sp", lamt[:, i, :], lam_d[i:i + 1, :].partition_broadcast(128), w=[tl])
        lw = cx.sb([128, 8], F32, "lw")
        ljunk = cx.sb([128, 64], F32, "ljunk")
        tlw = T()
        sc.c("dve", lambda e: e.tensor_tensor(ljunk[:], lamt[:, 0, :], lamt[:, 1, :], ALU.mult), r=[tl], w=[tlw])
        sc.c("dve", lambda e: e.reduce_sum(lw[:, 0:1], ljunk[:], axis=AX.X), r=[tlw], w=[tlw])
        sc.c("dve", lambda e: e.tensor_tensor(ljunk[:], lamt[:, 2, :], lamt[:, 3, :], ALU.mult), r=[tl, tlw], w=[tlw])
        sc.c("dve", lambda e: e.reduce_sum(lw[:, 1:2], ljunk[:], axis=AX.X), r=[tlw], w=[tlw])
        sc.c("act", lambda e: e.activation(lw[:, 2:4], lw[:, 0:2], AF.Exp), r=[tlw], w=[tlw])
        sc.c("dve", lambda e: e.tensor_tensor(lw[:, 4:5], lw[:, 3:4], lw[:, 2:3], ALU.subtract), r=[tlw], w=[tlw])
        lin = cx.sb([128, 2], F32, "lin")
        sc.dma("sp", lin[:], lin_d[0:1, :].partition_broadcast(128), w=[tlw])
        sc.c("dve", lambda e: e.tensor_scalar(lw[:, 5:6], lw[:, 4:5], lin[:, 0:1], None, ALU.add), r=[tlw], w=[tlw])
        neglam = lw[:, 5:6]
        sw = cx.sb([128, 2], F32, "sw")
        tsw = T()
        sc.dma("sp", sw[:, 0:1], subln_d[:, :], w=[tsw])
        sc.c("dve", lambda e: e.tensor_scalar(sw[:, 1:2], sw[:, 0:1], lin[:, 1:2], None, ALU.mult), r=[tsw, tlw], w=[tsw])
        wsc = sw[:, 1:2]
        epst = cx.sb([128, 1], F32, "epst")
        sc.c("dve", lambda e: e.memset(epst[:], RMS_EPS), w=[tsw])

        kT = [cx.sb([72, S], BF16, "kT") for _ in range(2)]
        tk = T()
        V = cx.sb([128, NKB, 128], BF16, "V")
        tV = T()
        Dt = cx.sb([128, 128], BF16, "Dt")
        tD = T()
        qL = [[cx.sb([72, 512], BF16, "qL") for _ in range(2)] for _ in range(2)]
        qR = [[cx.sb([72, 512], BF16, "qR") for _ in range(2)] for _ in range(2)]
        tq = [T() for _ in range(2)]
        Sp = [cx.ps([128, 1024], F32, "Sp") for _ in range(2)]
        tSp = [T() for _ in range(2)]
        Op = [cx.ps([128, 512], F32, "Op") for _ in range(2)]
        Lp = [cx.ps([128, 512], F32, "Lp") for _ in range(2)]
        tOL = [T() for _ in range(2)]
        tLp = [T() for _ in range(2)]
        NE = 6
        SPL = 352
        accLs = [cx.sb([128, 1024], F32, "accL") for _ in range(2)]
        taccDs, taccPs = [T(), T()], [T(), T()]
        ocs = [[cx.sb([128, 512], F32, "oc") for _ in range(2)] for _ in range(2)]
        tocs = [T(), T()]
        l1cs = [cx.sb([128, 512], F32, "l1c") for _ in range(2)]
        pending = []
        E = [cx.sb([128, 1024], BF16, "E") for _ in range(NE)]
        tE = [T() for _ in range(NE)]
        rl = cx.sb([128, 512], F32, "rl")
        o0 = cx.sb([128, 512], F32, "o0")
        o1 = cx.sb([128, 512], F32, "o1")
        sq = cx.sb([128, 512], F32, "sq")
        rs = cx.sb([128, 512], F32, "rs")
        tf = T()
        outb = [cx.sb([128, 512], BF16, "outb") for _ in range(2)]
        tout = [T() for _ in range(2)]
        nS = 0
        nE = 0
        items = [(hh, qb) for hh in range(4) for qb in range(NQB)]

        def load_q(i):
            hh, qb = items[i]
            b = i % 2
            cols = slice(qb * 512, (qb + 1) * 512)
            for c in range(2):
                sc.c("pool", lambda e: e.memset(qL[b][c][64:72, :], 0.0), w=[tq[b]])
                sc.c("pool", lambda e: e.memset(qR[b][c][64:72, :], 0.0), w=[tq[b]])
            for c in range(2):
                sc.dma("sp", qL[b][c][0:64, :], fm_d[hh, c * 64:(c + 1) * 64, cols], w=[tq[b]])
                sc.dma("sp", qR[b][c][0:64, :], fm_d[hh, c * 64:(c + 1) * 64, cols], w=[tq[b]])
                sc.dma("pool", qL[b][c][64:68, :], qtab_d[hh, :, cols], w=[tq[b]])
                sc.dma("pool", qR[b][c][68:72, :], qtab_d[hh, :, cols], w=[tq[b]])

        load_q(0)
        for it, (hh, qb) in enumerate(items):
            if qb == 0:
                for c in range(2):
                    sc.dma("sp", kT[c][0:64, :], fm_d[4 + hh, c * 64:(c + 1) * 64, :], w=[tk])
                    sc.dma("pool", kT[c][64:72, :], ktab_d[hh, :, :], w=[tk])
                sc.dma("sp", V[:], v_d[:, hh * 128:(hh + 1) * 128].rearrange("(kb p) d -> p kb d", p=128), w=[tV])
                sc.dma("pool", Dt[:], dtab_d[hh, :, :], w=[tD])
            if it + 1 < len(items):
                load_q(it + 1)
            if True:
                b = it % 2
                cols = slice(qb * 512, (qb + 1) * 512)
                def emitS(kb):
                    ks = slice(kb * 128, (kb + 1) * 128)
                    sp_, tsp_ = Sp[kb % 2], tSp[kb % 2]
                    for c in range(2):
                        co = c * 512
                        if kb < 4 * qb:
                            sc.c("pe", lambda e: e.matmul(sp_[:, co:co + 512], kT[c][0:72, ks], qL[b][c][0:72, :], start=True, stop=True),
                                 r=[tk, tq[b]], w=[tsp_])
                        elif kb > 4 * qb + 3:
                            sc.c("pe", lambda e: e.matmul(sp_[:, co:co + 512], kT[c][0:72, ks], qR[b][c][0:72, :], start=True, stop=True),
                                 r=[tk, tq[b]], w=[tsp_])
                        else:
                            t = kb - 4 * qb
                            for u in range(4):
                                us = slice(u * 128, (u + 1) * 128)
                                ps_ = slice(co + u * 128, co + (u + 1) * 128)
                                if u < t:
                                    sc.c("pe", lambda e: e.matmul(sp_[:, ps_], kT[c][0:72, ks], qR[b][c][0:72, us], start=True, stop=True),
                                         r=[tk, tq[b]], w=[tsp_])
                                elif u > t:
                                    sc.c("pe", lambda e: e.matmul(sp_[:, ps_], kT[c][0:72, ks], qL[b][c][0:72, us], start=True, stop=True),
                                         r=[tk, tq[b]], w=[tsp_])
                                else:
                                    sc.c("pe", lambda e: e.matmul(sp_[:, ps_], kT[c][0:64, ks], qL[b][c][0:64, us], start=True, stop=False),
                                         r=[tk, tq[b]], w=[tsp_])
                                    sc.c("pe", lambda e: e.matmul(sp_[:, ps_], ident[:], Dt[:], start=False, stop=True),
                                         r=[tI, tD], w=[tsp_])

                def emitRest(kb):
                    nonlocal nE
                    sp_, tsp_ = Sp[kb % 2], tSp[kb % 2]
                    e_, te_ = E[nE % NE], tE[nE % NE]
                    nE += 1
                    sc.c("act", lambda e: e.activation(e_[:], sp_[:], AF.Exp), r=[tsp_], w=[te_])
                    for c in range(2):
                        es = slice(c * 512, (c + 1) * 512)
                        sc.c("pe", lambda e: e.matmul(Op[c][:], V[:, kb, :], e_[:, es], start=(kb == KB0), stop=(kb == KB1)),
                             r=[tV, te_], w=[tOL[c]])
                    sc.c("pe", lambda e: e.matmul(Lp[1][:], ones[:], e_[:, 512:1024], start=(kb == KB0), stop=(kb == KB1)),
                         r=[tC, te_], w=[tLp[1]])
                    if kb == KB0:
                        sc.c("dve", lambda e: e.tensor_copy(accL[:, 0:SPL], e_[:, 0:SPL]), r=[te_], w=[taccD])
                        sc.c("pool", lambda e: e.tensor_copy(accL[:, SPL:512], e_[:, SPL:512]), r=[te_], w=[taccP])
                    else:
                        sc.c("dve", lambda e: e.tensor_tensor(accL[:, 0:SPL], accL[:, 0:SPL], e_[:, 0:SPL], ALU.add), r=[te_, taccD], w=[taccD])
                        sc.c("pool", lambda e: e.tensor_tensor(accL[:, SPL:512], accL[:, SPL:512], e_[:, SPL:512], ALU.add), r=[te_, taccP], w=[taccP])

                par = it % 2
                accL, taccD, taccP = accLs[par], taccDs[par], taccPs[par]
                if win_slopes is None:
                    kbs = list(range(NKB))
                else:
                    m_ = win_slopes[hh]
                    i0, i1 = qb * 512, qb * 512 + 511
                    kbs = [k_ for k_ in range(NKB) if m_ * max(0, k_ * 128 - i1, i0 - (k_ * 128 + 127)) <= win_th]
                KB0, KB1 = kbs[0], kbs[-1]
                emitS(kbs[0])
                step = max(1, (len(kbs) - 4) // 14)
                for idx, kb in enumerate(kbs):
                    if idx + 1 < len(kbs):
                        emitS(kbs[idx + 1])
                    emitRest(kb)
                    if pending and idx >= 2 and (idx - 2) % step == 0:
                        pending.pop(0)()
                while pending:
                    pending.pop(0)()
                oc, toc = ocs[par], tocs[par]
                for c in range(2):
                    sc.c("dve", lambda e: e.tensor_copy(oc[c][:], Op[c][:]), r=[tOL[c]], w=[toc])
                l1c = l1cs[par]
                sc.c("dve", lambda e: e.tensor_copy(l1c[:], Lp[1][:]), r=[tLp[1]], w=[toc])

                def mk_final(hh=hh, qb=qb, cols=cols, accL=accL, taccD=taccD, taccP=taccP, oc=oc, toc=toc, l1c=l1c):
                    ops = []
                    ops.append(lambda: sc.c("pe", lambda e: e.matmul(Lp[0][:], onesf[:], accL[:, 0:512], start=True, stop=True),
                                            r=[tC, taccD, taccP], w=[tLp[0]]))
                    ops.append(lambda: sc.c("dve", lambda e: e.reciprocal(rl[:], Lp[0][:]), r=[tLp[0]], w=[tf]))
                    ops.append(lambda: sc.c("dve", lambda e: e.tensor_tensor(o0[:], oc[0][:], rl[:], ALU.mult), r=[toc, tf], w=[tf]))
                    ops.append(lambda: sc.c("dve", lambda e: e.reciprocal(rl[:], l1c[:]), r=[toc, tf], w=[tf]))
                    ops.append(lambda: sc.c("dve", lambda e: e.tensor_tensor(o1[:], oc[1][:], rl[:], ALU.mult), r=[toc, tf], w=[tf]))
                    ops.append(lambda: sc.c("dve", lambda e: e.scalar_tensor_tensor(o0[:], o1[:], neglam, o0[:], ALU.mult, ALU.add), r=[tf, tlw], w=[tf]))
                    ops.append(lambda: sc.c("act", lambda e: e.activation(sq[:], o0[:], AF.Square), r=[tf], w=[tf]))
                    ops.append(lambda: sc.c("pe", lambda e: e.matmul(Lp[0][:], onesf[:], sq[:], start=True, stop=True), r=[tC, tf], w=[tLp[0]]))
                    ops.append(lambda: sc.c("act", lambda e: e.activation(rs[:], Lp[0][:], AF.Ln, bias=epst[:, 0:1], scale=1.0 / 128.0), r=[tLp[0], tsw], w=[tf]))
                    ops.append(lambda: sc.c("act", lambda e: e.activation(rs[:], rs[:], AF.Exp, scale=-0.5), r=[tf], w=[tf]))
                    ops.append(lambda: sc.c("dve", lambda e: e.tensor_tensor(o0[:], o0[:], rs[:], ALU.mult), r=[tf], w=[tf]))

                    def last():
                        ob_, tob_ = outb[qb % 2], tout[qb % 2]
                        sc.c("act", lambda e: e.activation(ob_[:], o0[:], AF.Copy, scale=wsc), r=[tf, tsw], w=[tob_])
                        sc.dma("sp", mix_d[hh * 128:(hh + 1) * 128, cols], ob_[:], r=[tob_])
                    ops.append(last)
                    return ops
                pending.extend(mk_final())
        while pending:
            pending.pop(0)()
        sc.barrier()


def phase_ssd(nc, sc, S, fm_d, z_d, dt_d, convw_d, convb_d, dtb_d, alog_d, dsk_d, normw_d, tri_d, ident_d,
              rows_d, mix_d, pfx="p3"):
    NCH = S // 128
    TP = min(512, S)
    with ExitStack() as es:
        cx = Ctx(nc, es, pfx)
        ident = cx.sb([128, 128], BF16, "ident")
        tri = cx.sb([128, 4, 128], F32, "tri")
        onesf = cx.sb([128, 128], F32, "onesf")
        tC = T()
        sc.dma("pool", ident[:], ident_d[:, :], w=[tC])
        for i in range(4):
            sc.dma("sp", tri[:, i, :], tri_d[i, :, :], w=[tC])
        sc.c("dve", lambda e: e.memset(onesf[:], 1.0), w=[tC])
        LE, LT, GE, GT = (tri[:, i, :] for i in range(4))
        cst = cx.sb([128, 64], F32, "cst")
        tcs = T()
        sc.c("dve", lambda e: e.memset(cst[:, 0:1], 1.0), w=[tcs])
        sc.c("dve", lambda e: e.memset(cst[:, 1:2], RMS_EPS), w=[tcs])
        one_c, eps_c = cst[:, 0:1], cst[:, 1:2]
        BTc = cx.sb([128, S], BF16, "BTc")
        CTc = cx.sb([128, S], BF16, "CTc")
        xtok = cx.sb([128, NCH, 256], BF16, "xtok")
        Btok = cx.sb([128, NCH, 128], BF16, "Btok")
        yf = cx.sb([128, NCH, 256], F32, "yf")
        tB, tCc, txt, tBt, tyf = T(), T(), T(), T(), T()
        raw = [cx.sb([128, TP + 4], BF16, "raw") for _ in range(2)]
        traw = [T(), T()]
        acc = [cx.sb([128, TP], F32, "acc") for _ in range(2)]
        tacc = [T(), T()]
        xsT = cx.sb([128, TP], BF16, "xsT")
        txs = T()
        cw = cx.sb([128, 8, 6], F32, "cw")
        tcw = T()
        for i in range(8):
            sc.dma("sp", cw[:, i, 0:5], convw_d[i, :, :], w=[tcw])
            sc.dma("sp", cw[:, i, 5:6], convb_d[i, :, :], w=[tcw])
        ptr = [cx.ps([128, 4, 128], BF16, "ptr") for _ in range(2)]
        tptr = [T(), T()]
        pst = [cx.ps([128, 256], F32, "pst") for _ in range(2)]
        tpst = [T(), T()]
        pcb = cx.ps([128, 256], F32, "pcb")
        tpcb = [T(), T()]
        pY = cx.ps([128, 256], F32, "pY")
        tpY = T()
        pYo = cx.ps([128, 256], F32, "pYo")
        tpYo = T()
        pS = cx.ps([128, 256], F32, "pS")
        tpS = T()
        dtr = cx.sb([128, NCH, 4], F32, "dtr")
        dts = cx.sb([128, NCH, 4], F32, "dts")
        dA = cx.sb([128, NCH, 4], F32, "dA")
        fac2 = cx.sb([128, NCH, 4], F32, "fac2")
        eA = cx.sb([128, NCH * 4], F32, "eA")
        eD = cx.sb([128, NCH * 4], F32, "eD")
        eT = cx.sb([128, NCH * 4], F32, "eT")
        sclr = cx.sb([128, NCH * 4], F32, "sclr")
        rowsb = cx.sb([128, 2, 128], F32, "rowsb")
        par = cx.sb([128, 5, 8], F32, "par")
        nw = cx.sb([128, 512], F32, "nw")
        tdt, tst, tpar = T(), T(), T()
        for i in range(2):
            sc.dma("sp", par[:, i, :], dtb_d[i:i + 1, :].partition_broadcast(128), w=[tpar])
            sc.dma("sp", par[:, 2 + i, :], alog_d[i:i + 1, :].partition_broadcast(128), w=[tpar])
        sc.dma("sp", par[:, 4, :], dsk_d[0:1, :].partition_broadcast(128), w=[tpar])
        sc.dma("sp", nw[:], normw_d[0:1, :].partition_broadcast(128), w=[tpar])
        sc.c("act", lambda e: e.activation(par[:, 2:4, :], par[:, 2:4, :], AF.Exp), r=[tpar], w=[tpar])
        sc.c("dve", lambda e: e.tensor_scalar(par[:, 2:4, :], par[:, 2:4, :], -1.0, None, ALU.mult), r=[tpar], w=[tpar])
        Rb = [cx.sb([128, 4, 128], F32, "Rb") for _ in range(3)]
        tRb = [T(), T(), T()]
        Ex = [cx.sb([128, 4, 128], F32, "Ex") for _ in range(2)]
        tEx = [T(), T()]
        cbm = [cx.sb([128, 128], F32, "cbm") for _ in range(2)]
        tcbm = [T(), T()]
        MT = [cx.sb([128, 4, 128], BF16, "MT") for _ in range(2)]
        tMT = [T(), T()]
        xdt = [cx.sb([128, 256], BF16, "xdt") for _ in range(2)]
        xdec = [cx.sb([128, 256], BF16, "xdec") for _ in range(2)]
        txd = [T(), T()]
        Ysb = cx.sb([128, 256], F32, "Ysb")
        tYs = T()
        yb = cx.sb([128, 256], F32, "yb")
        tyb = T()
        ST = cx.sb([128, 256], F32, "ST")
        STb = cx.sb([128, 256], BF16, "STb")
        tST, tSTb = T(), T()
        zt = [cx.sb([128, 256], BF16, "zt") for _ in range(2)]
        tz = [T(), T()]
        zbig = [cx.sb([128, 8, 256], BF16, "zbig") for _ in range(2)]
        tzbig = [T(), T()]
        dx4 = cx.sb([128, 4, 256], F32, "dx4")
        jk4 = cx.sb([128, 4, 256], BF16, "jk4")
        ob4 = cx.sb([128, 4, 256], BF16, "ob4")
        fs4 = cx.sb([128, 12], F32, "fs4")
        tdx4, tjk4, tob4, tfs4 = T(), T(), T(), T()
        tfin = T()
        oT = [cx.sb([128, 2, 512], BF16, "oT") for _ in range(2)]
        toT = [T(), T()]
        trows = T()
        trowd = [T(), T()]
        nraw = 0
        ntr = 0
        for g in range(2):
            for kind, cid in (("B", 12 + g), ("C", 14 + g), ("x0", 8 + 2 * g), ("x1", 9 + 2 * g)):
                for pc in range(S // TP):
                    t0 = pc * TP
                    rw, trw = raw[nraw % 2], traw[nraw % 2]
                    ac, tac = acc[nraw % 2], tacc[nraw % 2]
                    nraw += 1
                    lo = max(t0 - 2, 0)
                    hi = min(t0 + TP + 2, S)
                    if t0 == 0:
                        sc.c("pool", lambda e: e.memset(rw[:, 0:2], 0.0), w=[trw])
                    if t0 + TP == S:
                        sc.c("pool", lambda e: e.memset(rw[:, TP + 2:TP + 4], 0.0), w=[trw])
                    sc.dma("sp", rw[:, lo - (t0 - 2):hi - (t0 - 2)], fm_d[cid, :, lo:hi], w=[trw])
                    sc.c("dve", lambda e: e.tensor_scalar(ac[:], rw[:, 0:TP], cw[:, cid - 8, 0:1], None, ALU.mult), r=[trw, tcw], w=[tac])
                    for w_ in range(1, 5):
                        sc.c("dve", lambda e: e.scalar_tensor_tensor(ac[:], rw[:, w_:w_ + TP], cw[:, cid - 8, w_:w_ + 1], ac[:], ALU.mult, ALU.add),
                             r=[trw, tcw, tac], w=[tac])
                    if kind == "B":
                        sc.c("act", lambda e: e.activation(BTc[:, t0:t0 + TP], ac[:], AF.Silu, bias=cw[:, cid - 8, 5:6]), r=[tac, tcw], w=[tB])
                        src, tsrc = BTc, tB
                    elif kind == "C":
                        sc.c("act", lambda e: e.activation(CTc[:, t0:t0 + TP], ac[:], AF.Silu, bias=cw[:, cid - 8, 5:6]), r=[tac, tcw], w=[tCc])
                        continue
                    else:
                        sc.c("act", lambda e: e.activation(xsT[:], ac[:], AF.Silu, bias=cw[:, cid - 8, 5:6]), r=[tac, tcw], w=[txs])
                    for q4 in range(TP // 512):
                        p_, tp_ = ptr[ntr % 2], tptr[ntr % 2]
                        ntr += 1
                        for u in range(4):
                            c0 = q4 * 512 + u * 128
                            if kind == "B":
                                sc.c("pe", lambda e: e.transpose(p_[:, u, :], BTc[:, t0 + c0:t0 + c0 + 128], ident[:]), r=[tB, tC], w=[tp_])
                            else:
                                sc.c("pe", lambda e: e.transpose(p_[:, u, :], xsT[:, c0:c0 + 128], ident[:]), r=[txs, tC], w=[tp_])
                        ch0 = (t0 + q4 * 512) // 128
                        if kind == "B":
                            sc.c("act", lambda e: e.activation(Btok[:, ch0:ch0 + 4, :], p_[:], AF.Copy), r=[tp_], w=[tBt])
                        else:
                            half = 0 if kind == "x0" else 1
                            sc.c("act", lambda e: e.activation(xtok[:, ch0:ch0 + 4, half * 128:(half + 1) * 128], p_[:], AF.Copy), r=[tp_], w=[txt])
            tzd = []
            ZP = min(8, NCH)
            for zi in range(NCH // ZP):
                zb_, tzb_ = zbig[zi % 2], tzbig[zi % 2]
                zsrc = z_d[zi * ZP * 128:(zi + 1) * ZP * 128, g * 256:(g + 1) * 256].rearrange("(c l) d -> l c d", l=128)
                sc.dma("sp", zb_[:, 0:ZP, :], zsrc, w=[tzb_])
                sc.c("act", lambda e: e.activation(zb_[:, 0:ZP, :], zb_[:, 0:ZP, :], AF.Silu), r=[tzb_], w=[tzb_])
                tz1 = T()
                sc.dma("sp", zsrc, zb_[:, 0:ZP, :], r=[tzb_], w=[tz1])
                tzd.append(tz1)
            sc.dma("sp", dtr[:], dt_d[:, g * 4:(g + 1) * 4].rearrange("(c l) r -> l c r", l=128), w=[tdt])
            for d in range(2):
                fwd = (d == 0)
                for r_ in range(4):
                    sc.c("dve", lambda e: e.tensor_scalar(dts[:, :, r_], dtr[:, :, r_], par[:, d, g * 4 + r_:g * 4 + r_ + 1], None, ALU.add),
                         r=[tdt, tpar, tst], w=[tst])
                sc.c("act", lambda e: e.activation(dts[:], dts[:], AF.Exp), r=[tst], w=[tst])
                sc.c("act", lambda e: e.activation(dts[:], dts[:], AF.Ln, bias=one_c), r=[tst, tcs], w=[tst])
                for r_ in range(4):
                    sc.c("dve", lambda e: e.tensor_scalar(dA[:, :, r_], dts[:, :, r_], par[:, 2 + d, g * 4 + r_:g * 4 + r_ + 1], None, ALU.mult),
                         r=[tst, tpar], w=[tst])
                dAf = dA[:].rearrange("p c r -> p (c r)")
                M1 = LE if fwd else GE
                M2 = GT if fwd else LT
                MR = LE if fwd else LT
                for (mat, dst, keep) in ((M1, eA, fwd), (M2, eD, not fwd), (onesf[:], eT, False)):
                    for hf in range(NCH * 4 // 256 if NCH * 4 >= 256 else 1):
                        wd_ = min(256, NCH * 4)
                        p_, tp_ = pst[hf % 2], tpst[hf % 2]
                        sc.c("pe", lambda e: e.matmul(p_[:, 0:wd_], mat, dAf[:, hf * wd_:(hf + 1) * wd_], start=True, stop=True), r=[tC, tst], w=[tp_])
                        if keep:
                            sc.c("dve", lambda e: e.tensor_copy(sclr[:, hf * wd_:(hf + 1) * wd_], p_[:, 0:wd_]), r=[tp_], w=[tst])
                        sc.c("act", lambda e: e.activation(dst[:, hf * wd_:(hf + 1) * wd_], p_[:, 0:wd_], AF.Exp), r=[tp_], w=[tst])
                sc.c("dve", lambda e: e.tensor_tensor(fac2[:].rearrange("p c r -> p (c r)"), dts[:].rearrange("p c r -> p (c r)"), eD[:], ALU.mult), r=[tst], w=[tst])
                nrow = NCH * 4
                for hf in range((nrow + 127) // 128):
                    m_ = min(128, nrow - hf * 128)
                    p_, tp_ = pst[hf % 2], tpst[hf % 2]
                    sc.c("pe", lambda e: e.matmul(p_[0:m_, 0:128], dAf[:, hf * 128:hf * 128 + m_], MR, start=True, stop=True), r=[tC, tst], w=[tp_])
                    sc.c("dve", lambda e: e.tensor_copy(rowsb[0:m_, hf, :], p_[0:m_, 0:128]), r=[tp_], w=[trows])
                    sc.dma("sp", rows_d[d].rearrange("c (r l) -> (c r) l", l=128)[hf * 128:hf * 128 + m_, :], rowsb[0:m_, hf, :], r=[trows], w=[trowd[hf]])
                sc.c("dve", lambda e: e.memset(ST[:], 0.0), w=[tST])
                sc.c("dve", lambda e: e.memset(STb[:], 0.0), w=[tSTb])
                order = list(range(NCH)) if fwd else list(range(NCH - 1, -1, -1))
                v3 = lambda ap: ap.rearrange("p (r d) -> p r d", d=64)
                bc = lambda ap, n: ap.unsqueeze(2).to_broadcast([128, 4, n])

                def rb_load(n_):
                    ci = order[n_]
                    rb, trb = Rb[n_ % 3], tRb[n_ % 3]
                    sc.dma("sp", rb[:].rearrange("p r l -> p (r l)"), rows_d[d, ci:ci + 1, :].partition_broadcast(128),
                           r=trowd, w=[trb])

                def front(n_):
                    ci = order[n_]
                    k_ = n_ % 2
                    cs = slice(ci * 128, (ci + 1) * 128)
                    c4 = slice(ci * 4, (ci + 1) * 4)
                    rb, trb = Rb[n_ % 3], tRb[n_ % 3]
                    sc.c("dve", lambda e: e.tensor_tensor(rb[:], rb[:], bc(sclr[:, c4], 128), ALU.subtract), r=[tst], w=[trb])
                    sc.c("dve", lambda e: e.tensor_scalar(rb[:], rb[:], 0.0, None, ALU.min if fwd else ALU.max), w=[trb])
                    sc.c("act", lambda e: e.activation(Ex[k_][:], rb[:], AF.Exp, scale=1.0 if fwd else -1.0), r=[trb], w=[tEx[k_]])
                    sc.c("pe", lambda e: e.matmul(pcb[:, k_ * 128:(k_ + 1) * 128], BTc[:, cs], CTc[:, cs], start=True, stop=True), r=[tB, tCc], w=[tpcb[k_]])
                    sc.c("dve", lambda e: e.tensor_tensor(cbm[k_][:], pcb[:, k_ * 128:(k_ + 1) * 128], LE if fwd else GE, ALU.mult), r=[tpcb[k_], tC], w=[tcbm[k_]])
                    sc.c("dve", lambda e: e.tensor_tensor(MT[k_][:], Ex[k_][:], cbm[k_][:].unsqueeze(1).to_broadcast([128, 4, 128]), ALU.mult),
                         r=[tEx[k_], tcbm[k_]], w=[tMT[k_]])
                    sc.c("pool", lambda e: e.tensor_tensor(v3(xdt[k_][:]), v3(xtok[:, ci, :]), bc(dts[:, ci, :], 64), ALU.mult), r=[txt, tst], w=[txd[k_]])
                    sc.c("pool", lambda e: e.tensor_tensor(v3(xdec[k_][:]), v3(xtok[:, ci, :]), bc(fac2[:, ci, :], 64), ALU.mult), r=[txt, tst], w=[txd[k_]])

                rb_load(0)
                if NCH > 1:
                    rb_load(1)
                front(0)
                for n_, ci in enumerate(order):
                    if n_ + 2 < NCH:
                        rb_load(n_ + 2)
                    if n_ + 1 < NCH:
                        front(n_ + 1)
                    k_ = n_ % 2
                    cs = slice(ci * 128, (ci + 1) * 128)
                    c4 = slice(ci * 4, (ci + 1) * 4)
                    for r_ in range(4):
                        rs_ = slice(r_ * 64, (r_ + 1) * 64)
                        sc.c("pe", lambda e: e.matmul(pY[:, rs_], MT[k_][:, r_, :], xdt[k_][:, rs_], start=True, stop=True), r=[tMT[k_], txd[k_]], w=[tpY])
                    sc.c("pe", lambda e: e.matmul(pYo[:], CTc[:, cs], STb[:], start=True, stop=True), r=[tCc, tSTb], w=[tpYo])
                    ydst, tyd = (yf[:, ci, :], tyf) if fwd else (yb[:], tyb)
                    sc.c("dve", lambda e: e.tensor_tensor(v3(Ysb[:]), v3(pYo[:]), bc(eA[:, c4], 64), ALU.mult), r=[tpYo, tst], w=[tYs])
                    sc.c("dve", lambda e: e.tensor_tensor(ydst, Ysb[:], pY[:], ALU.add), r=[tYs, tpY], w=[tyd])
                    sc.c("pe", lambda e: e.matmul(pS[:], Btok[:, ci, :], xdec[k_][:], start=True, stop=True), r=[tBt, txd[k_]], w=[tpS])
                    sc.c("dve", lambda e: e.tensor_tensor(v3(ST[:]), v3(ST[:]), bc(eT[:, c4], 64), ALU.mult), r=[tst, tST], w=[tST])
                    sc.c("dve", lambda e: e.tensor_tensor(ST[:], ST[:], pS[:], ALU.add), r=[tpS, tST], w=[tST])
                    sc.c("act", lambda e: e.activation(STb[:], ST[:], AF.Copy), r=[tST], w=[tSTb])
                    if not fwd:
                        sc.c("dve", lambda e: e.tensor_tensor(yf[:, ci, :], yf[:, ci, :], yb[:], ALU.add), r=[tyb, tyf], w=[tyf])
            v4 = lambda ap: ap.rearrange("p c (r d) -> p c r d", d=64)
            for blk in range(NCH // 4 if NCH >= 4 else 1):
                nb_ = min(4, NCH)
                c0_ = blk * nb_
                zb_, tzb_ = zbig[blk % 2], tzbig[blk % 2]
                sc.dma("sp", zb_[:, 0:nb_, :], z_d[c0_ * 128:(c0_ + nb_) * 128, g * 256:(g + 1) * 256].rearrange("(c l) d -> l c d", l=128),
                       r=tzd, w=[tzb_])
                yv = yf[:, c0_:c0_ + nb_, :]
                dsk_b = par[:, 4, g * 4:(g + 1) * 4].unsqueeze(1).unsqueeze(3).to_broadcast([128, nb_, 4, 64])
                sc.c("pool", lambda e: e.tensor_tensor(v4(dx4[:, 0:nb_, :]), v4(xtok[:, c0_:c0_ + nb_, :]), dsk_b, ALU.mult), r=[txt, tpar], w=[tdx4])
                sc.c("dve", lambda e: e.tensor_tensor(yv, yv, dx4[:, 0:nb_, :], ALU.add), r=[tdx4, tyf], w=[tyf])
                sc.c("dve", lambda e: e.tensor_tensor(dx4[:, 0:nb_, :], yv, zb_[:, 0:nb_, :], ALU.mult), r=[tyf, tzb_, tdx4], w=[tdx4])
                sc.c("act", lambda e: e.activation(jk4[:, 0:nb_, :], dx4[:, 0:nb_, :], AF.Square), r=[tdx4], w=[tjk4])
                sc.c("dve", lambda e: e.reduce_sum(fs4[:, 0:nb_], jk4[:, 0:nb_, :], axis=AX.X), r=[tjk4], w=[tfs4])
                sc.c("act", lambda e: e.activation(fs4[:, 4:4 + nb_], fs4[:, 0:nb_], AF.Ln, bias=eps_c, scale=1.0 / 256.0), r=[tfs4, tcs], w=[tfs4])
                sc.c("act", lambda e: e.activation(fs4[:, 8:8 + nb_], fs4[:, 4:4 + nb_], AF.Exp, scale=-0.5), r=[tfs4], w=[tfs4])
                sc.c("dve", lambda e: e.tensor_tensor(dx4[:, 0:nb_, :], dx4[:, 0:nb_, :], fs4[:, 8:8 + nb_].unsqueeze(2).to_broadcast([128, nb_, 256]), ALU.mult),
                     r=[tfs4, tdx4], w=[tdx4])
                sc.c("dve", lambda e: e.tensor_tensor(ob4[:, 0:nb_, :], dx4[:, 0:nb_, :], nw[:, g * 256:(g + 1) * 256].unsqueeze(1).to_broadcast([128, nb_, 256]), ALU.mult),
                     r=[tdx4, tpar], w=[tob4])
                o_, to_ = oT[blk % 2], toT[blk % 2]
                for hf in range(2):
                    p_, tp_ = ptr[hf], tptr[hf]
                    for u in range(nb_):
                        sc.c("pe", lambda e: e.transpose(p_[:, u, :], ob4[:, u, hf * 128:(hf + 1) * 128], ident[:]), r=[tob4, tC], w=[tp_])
                    sc.c("act", lambda e: e.activation(o_[:, hf, 0:nb_ * 128].rearrange("p (u t) -> p u t", t=128), p_[:, 0:nb_, :], AF.Copy), r=[tp_], w=[to_])
                for hf in range(2):
                    r0 = 512 + g * 256 + hf * 128
                    sc.dma("sp", mix_d[r0:r0 + 128, c0_ * 128:(c0_ + nb_) * 128], o_[:, hf, 0:nb_ * 128], r=[to_])
        sc.barrier()


def g_off(x):
    return 0


def _layernorm(sc, cx_tiles, r, tr, g, b, tgb, out, tout):
    st, junk, tst = cx_tiles
    sc.c("dve", lambda e: e.reduce_sum(st[:, 0:1], r[:], axis=AX.X), r=[tr], w=[tst])
    sc.c("dve", lambda e: e.tensor_scalar(st[:, 1:2], st[:, 0:1], -1.0 / 1024.0, None, ALU.mult), r=[tst], w=[tst])
    sc.c("dve", lambda e: e.memset(st[:, 2:3], 0.0), w=[tst])
    sc.c("act", lambda e: e.activation(junk[:], r[:], AF.Square, bias=st[:, 1:2], accum_out=st[:, 2:3]), r=[tr, tst], w=[tst])
    sc.c("act", lambda e: e.activation(st[:, 3:4], st[:, 2:3], AF.Sqrt, bias=st[:, 5:6], scale=1.0 / 1024.0), r=[tst], w=[tst])
    sc.c("dve", lambda e: e.reciprocal(st[:, 4:5], st[:, 3:4]), r=[tst], w=[tst])
    sc.c("dve", lambda e: e.tensor_scalar(r[:], r[:], st[:, 1:2], st[:, 4:5], ALU.add, ALU.mult), r=[tst, tr], w=[tr])
    sc.c("dve", lambda e: e.tensor_tensor(r[:], r[:], g[:], ALU.mult), r=[tr, tgb], w=[tr])
    sc.c("dve", lambda e: e.tensor_tensor(out[:], r[:], b[:], ALU.add), r=[tr, tgb], w=[tout])


def phase_outproj(nc, sc, NT, x_d, mixT_d, wo_d, g_d, b_d, ident_d, x1_d, x1T_d, pfx="p4a", dyn=None, prefetch=None):
    with ExitStack() as es:
        cx = Ctx(nc, es, pfx)
        Wo = cx.sb([128, 16, 1024], BF16, "Wo")
        tW = T()
        wv = wo_d.rearrange("(c p) n -> p c n", p=128)
        for c in range(16):
            sc.dma("pool", Wo[:, c, :], wv[:, c, :], w=[tW])
        ident = cx.sb([128, 128], BF16, "ident")
        sc.dma("pool", ident[:], ident_d[:, :], w=[tW])
        if prefetch is not None:
            tpf = T()
            for (Wt, wd_) in prefetch:
                wv2 = wd_.rearrange("(c p) n -> p c n", p=128)
                for c in range(8):
                    sc.dma("pool", Wt[:, c, :], wv2[:, c, :], w=[tpf])
        g = cx.sb([128, 1024], F32, "g")
        b = cx.sb([128, 1024], F32, "b")
        tgb = T()
        sc.dma("sp", g[:], g_d[0:1, :].partition_broadcast(128), w=[tgb])
        sc.dma("sp", b[:], b_d[0:1, :].partition_broadcast(128), w=[tgb])
        st = cx.sb([128, 8], F32, "st")
        junk = cx.sb([128, 1024], F32, "junk")
        tst = T()
        sc.c("dve", lambda e: e.memset(st[:, 5:6], LN_EPS), w=[tst])
        mixT = [cx.sb([128, 16, 512], BF16, "mixT") for _ in range(2)]
        tmx = [T(), T()]
        xres = [cx.sb([128, 1024], F32, "xres") for _ in range(2)]
        txr = [T(), T()]
        rt = [cx.sb([128, 1024], F32, "rt") for _ in range(2)]
        trt = [T(), T()]
        x1 = [cx.sb([128, 1024], F32, "x1") for _ in range(2)]
        tx1 = [T(), T()]
        x1b = cx.sb([128, 1024], BF16, "x1b")
        tx1b = T()
        x1T = [cx.sb([128, 8, 512], BF16, "x1T")] * 2
        tx1T = [T()] * 2
        po = [cx.ps([128, 1024], F32, "po") for _ in range(2)]
        tpo = [T(), T()]
        ptr = [cx.ps([128, 8, 128], BF16, "ptr") for _ in range(2)]
        tptr = [T(), T()]
        n = 0
        NB = NT // 512
        if dyn is None:
            mv = mixT_d.rearrange("(c p) t -> p c t", p=128)

            def load_mix(blk_, dst, tdst):
                sc.dma("sp", dst[:], mv[:, :, blk_ * 512:(blk_ + 1) * 512], w=[tdst])
        else:
            gath_h, SG, reg_p, reg_t = dyn

            def load_mix(blk_, dst, tdst):
                for gi, (rank, rb) in enumerate(((0, 0), (1, 0), (0, 512), (1, 512))):
                    off = (rank * 1024 + rb) * SG + blk_ * 512
                    nc.gpsimd.reg_add(reg_t, reg_p, off)
                    sc.dma("pool", dst[:, gi * 4:(gi + 1) * 4, :], bass.AP(gath_h, reg_t, [[SG, 128], [128 * SG, 4], [1, 512]]), w=[tdst])
        load_mix(0, mixT[0], tmx[0])
        NTILE = NB * 4

        def mm(n_):
            blk_, s_ = divmod(n_, 4)
            mx, tm = mixT[blk_ % 2], tmx[blk_ % 2]
            p_, tp_ = po[n_ % 2], tpo[n_ % 2]
            for hf in range(2):
                for c in range(16):
                    sc.c("pe", lambda e: e.matmul(p_[:, hf * 512:(hf + 1) * 512], mx[:, c, s_ * 128:(s_ + 1) * 128], Wo[:, c, hf * 512:(hf + 1) * 512],
                                                 start=(c == 0), stop=(c == 15)), r=[tm, tW], w=[tp_])

        mm(0)
        for n in range(NTILE):
            blk, s = divmod(n, 4)
            if s == 0 and blk + 1 < NB:
                load_mix(blk + 1, mixT[(blk + 1) % 2], tmx[(blk + 1) % 2])
            xT_, txT_ = x1T[blk % 2], tx1T[blk % 2]
            t0 = blk * 512 + s * 128
            xr, txr_ = xres[n % 2], txr[n % 2]
            r_, tr_ = rt[n % 2], trt[n % 2]
            x1_, tx1_ = x1[n % 2], tx1[n % 2]
            p_, tp_ = po[n % 2], tpo[n % 2]
            q_, tq_ = ptr[n % 2], tptr[n % 2]
            sc.dma("sp", xr[:], x_d[t0:t0 + 128, :], w=[txr_])
            if n + 1 < NTILE:
                mm(n + 1)
            for hf in range(2):
                hs = slice(hf * 512, (hf + 1) * 512)
                sc.c("dve", lambda e: e.scalar_tensor_tensor(r_[:, hs], xr[:, hs], ALPHA, p_[:, hs], ALU.mult, ALU.add), r=[txr_, tp_], w=[tr_])
            _layernorm(sc, (st, junk, tst), r_, tr_, g, b, tgb, x1_, tx1_)
            sc.dma("sp", x1_d[t0:t0 + 128, :], x1_[:], r=[tx1_])
            sc.c("act", lambda e: e.activation(x1b[:], x1_[:], AF.Copy), r=[tx1_], w=[tx1b])
            for c in range(8):
                sc.c("pe", lambda e: e.transpose(q_[:, c, :], x1b[:, c * 128:(c + 1) * 128], ident[:]), r=[tx1b, tW], w=[tq_])
            sc.c("act", lambda e: e.activation(xT_[:, :, s * 128:(s + 1) * 128], q_[:], AF.Copy), r=[tq_], w=[txT_])
            if s == 3:
                sc.dma("sp", x1T_d.rearrange("(c p) t -> p c t", p=128)[:, :, blk * 512:(blk + 1) * 512], xT_[:], r=[txT_])
        sc.barrier()


def phase_ffn(nc, sc, NT, x1_d, x1T_d, wg_d, wu_d, wd_d, g_d, b_d, out_d, pfx="p4b", pre=None):
    NF = D_FF // 128
    with ExitStack() as es:
        cx = Ctx(nc, es, pfx)
        tW = T()
        if pre is None:
            Wg = cx.sb([128, 8, D_FF], BF16, "Wg")
            Wu = cx.sb([128, 8, D_FF], BF16, "Wu")
            for (Wt, wd_) in ((Wg, wg_d), (Wu, wu_d)):
                wv = wd_.rearrange("(c p) n -> p c n", p=128)
                for c in range(8):
                    sc.dma("pool", Wt[:, c, :], wv[:, c, :], w=[tW])
        else:
            Wg, Wu = pre
        Wd = cx.sb([128, NF, 1024], BF16, "Wd")
        wv = wd_d.rearrange("(c p) n -> p c n", p=128)
        for c in range(NF):
            sc.dma("pool", Wd[:, c, :], wv[:, c, :], w=[tW])
        g = cx.sb([128, 1024], F32, "g")
        b = cx.sb([128, 1024], F32, "b")
        tgb = T()
        sc.dma("sp", g[:], g_d[0:1, :].partition_broadcast(128), w=[tgb])
        sc.dma("sp", b[:], b_d[0:1, :].partition_broadcast(128), w=[tgb])
        st = cx.sb([128, 8], F32, "st")
        junk = cx.sb([128, 1024], BF16, "junk")
        tst = T()
        sc.c("dve", lambda e: e.memset(st[:, 5:6], LN_EPS), w=[tst])
        x1T = [cx.sb([128, 8, 512], BF16, "x1T")] * 2
        tx1T = [T()] * 2
        hid = cx.sb([128, NF, 512], BF16, "hid")
        thid = T()
        sg = [cx.sb([128, 512], F32, "sg") for _ in range(2)]
        tsg = [T(), T()]
        x1 = [cx.sb([128, 1024], F32, "x1")] * 2
        tx1 = [T()] * 2
        rt = [cx.sb([128, 1024], F32, "rt") for _ in range(1)] * 2
        trt = [T()] * 2
        ot = [cx.sb([128, 1024], F32, "ot") for _ in range(1)] * 2
        tot = [T()] * 2
        pg = [cx.ps([128, 512], F32, "pg") for _ in range(2)]
        pu = [cx.ps([128, 512], F32, "pu") for _ in range(2)]
        tpg = [T(), T()]
        tpu = [T(), T()]
        po = [cx.ps([128, 1024], F32, "po") for _ in range(2)]
        tpo = [T(), T()]
        NB = NT // 512
        xv = x1T_d.rearrange("(c p) t -> p c t", p=128)
        n = 0
        nf = 0
        for blk in range(NB):
            xT_, txT_ = x1T[blk % 2], tx1T[blk % 2]
            sc.dma("sp", xT_[:], xv[:, :, blk * 512:(blk + 1) * 512], w=[txT_])
            for f in range(NF):
                pg_, tpg_ = pg[nf % 2], tpg[nf % 2]
                pu_, tpu_ = pu[nf % 2], tpu[nf % 2]
                sg_, tsg_ = sg[nf % 2], tsg[nf % 2]
                nf += 1
                for c in range(8):
                    sc.c("pe", lambda e: e.matmul(pg_[:], Wg[:, c, f * 128:(f + 1) * 128], xT_[:, c, :], start=(c == 0), stop=(c == 7)), r=[tW, txT_], w=[tpg_])
                for c in range(8):
                    sc.c("pe", lambda e: e.matmul(pu_[:], Wu[:, c, f * 128:(f + 1) * 128], xT_[:, c, :], start=(c == 0), stop=(c == 7)), r=[tW, txT_], w=[tpu_])
                sc.c("act", lambda e: e.activation(sg_[:], pg_[:], AF.Silu), r=[tpg_], w=[tsg_])
                sc.c("dve", lambda e: e.tensor_tensor(hid[:, f, :], sg_[:], pu_[:], ALU.mult), r=[tsg_, tpu_], w=[thid])
            for s in range(4):
                t0 = blk * 512 + s * 128
                x1_, tx1_ = x1[n % 2], tx1[n % 2]
                r_, tr_ = rt[n % 2], trt[n % 2]
                o_, to_ = ot[n % 2], tot[n % 2]
                p_, tp_ = po[n % 2], tpo[n % 2]
                n += 1
                sc.dma("sp", x1_[:], x1_d[t0:t0 + 128, :], w=[tx1_])
                for hf in range(2):
                    for f in range(NF):
                        sc.c("pe", lambda e: e.matmul(p_[:, hf * 512:(hf + 1) * 512], hid[:, f, s * 128:(s + 1) * 128], Wd[:, f, hf * 512:(hf + 1) * 512],
                                                     start=(f == 0), stop=(f == NF - 1)), r=[thid, tW], w=[tp_])
                for hf in range(2):
                    hs = slice(hf * 512, (hf + 1) * 512)
                    sc.c("dve", lambda e: e.scalar_tensor_tensor(r_[:, hs], x1_[:, hs], ALPHA, p_[:, hs], ALU.mult, ALU.add), r=[tx1_, tp_], w=[tr_])
                _layernorm(sc, (st, junk, tst), r_, tr_, g, b, tgb, o_, to_)
                sc.dma("pool", out_d[t0:t0 + 128, :], o_[:], r=[to_])
        sc.barrier()


from concourse.bass_utils import run_bass_kernel_spmd
import math

I32 = mybir.dt.int32
SEQ = 8192
BATCH = 4
NCORES = 8
HALF = SEQ // 2
DEPTH = 2
PAIRS = [[0, 1], [2, 3], [4, 5], [6, 7]]
HEADS_P = [[7, 5, 3, 1], [6, 4, 2, 0]]
WIN_SLOPES = [2.0 ** -(h + 1) for h in HEADS_P[0]]


def _const_tables(S, heads):
    pos = np.arange(S)
    r = (pos % 128).astype(np.float32)
    a = (pos // 128).astype(np.float32)
    kt = np.zeros((4, 8, S), np.float32)
    qt = np.zeros((4, 4, S), np.float32)
    dt = np.zeros((4, 128, 128), np.float32)
    kk = np.arange(128)[:, None]
    qq = np.arange(128)[None, :]
    for i, h in enumerate(heads):
        m = 2.0 ** (-(h + 1))
        kt[i, 0] = 1
        kt[i, 1] = 1
        kt[i, 2] = m * r
        kt[i, 3] = m * 128 * a
        kt[i, 4:8] = -kt[i, 0:4]
        qt[i, 0] = -m * r
        qt[i, 1] = -m * 128 * a
        qt[i, 2] = 1
        qt[i, 3] = 1
        dt[i] = -m * np.abs(qq - kk)
    return kt, qt, dt


def _tri_tables():
    k = np.arange(128)[:, None]
    j = np.arange(128)[None, :]
    return np.stack([(k <= j), (k < j), (k >= j), (k > j)]).astype(np.float32)


def build_fused(S=SEQ, depth=DEPTH):
    NT = S // 2
    nc = bass.Bass("TRN2", target_bir_lowering=False)
    I = lambda n, sh: nc.dram_tensor(n, sh, F32, kind="ExternalInput").ap()
    x_full = I("x_full", [S, 1024])
    x_own = I("x_own", [NT, 1024])
    pid = nc.dram_tensor("pid", [1, 1], I32, kind="ExternalInput").ap()
    ident = I("ident", [128, 128])
    ktab = I("ktab", [4, 8, S])
    qtab = I("qtab", [4, 4, S])
    dtab = I("dtab", [4, 128, 128])
    tri = I("tri", [4, 128, 128])
    L = []
    for l in range(depth):
        L.append(dict(
            w=I(f"w{l}", [1024, NCOL]), lam=I(f"lam{l}", [4, 64]), subln=I(f"subln{l}", [128, 1]), lin=I(f"lin{l}", [1, 2]),
            convw=I(f"convw{l}", [8, 128, 5]), convb=I(f"convb{l}", [8, 128, 1]), dtb=I(f"dtb{l}", [2, 8]),
            alog=I(f"alog{l}", [2, 8]), dsk=I(f"dsk{l}", [1, 8]), normw=I(f"normw{l}", [1, 512]),
            wo=I(f"wo{l}", [2048, 1024]), g1=I(f"g1{l}", [1, 1024]), b1=I(f"b1{l}", [1, 1024]),
            g2=I(f"g2{l}", [1, 1024]), b2=I(f"b2{l}", [1, 1024]),
            wg=I(f"wg{l}", [1024, D_FF]), wu=I(f"wu{l}", [1024, D_FF]), wd=I(f"wd{l}", [D_FF, 1024])))
    out = nc.dram_tensor("out", [NT, 1024], F32, kind="ExternalOutput").ap()
    fm = nc.dram_tensor("fm", [16, 128, S], BF16).ap()
    v = nc.dram_tensor("v", [S, 512], BF16).ap()
    z = nc.dram_tensor("z", [S, 512], BF16).ap()
    dt = nc.dram_tensor("dt", [S, 8], F32).ap()
    rows = nc.dram_tensor("rows", [2, S // 128, 512], F32).ap()
    mix = nc.dram_tensor("mix", [1024, S], BF16)
    gath = [nc.dram_tensor(f"gath{l}", [8, 256, S], BF16) for l in range(depth)]
    x1 = nc.dram_tensor("x1", [NT, 1024], F32).ap()
    x1T = nc.dram_tensor("x1T", [1024, NT], BF16).ap()
    mixmine = nc.dram_tensor("mixmine", [2048, NT], BF16)
    xo = [nc.dram_tensor(f"xo{l}", [NT, 1024], F32) for l in range(depth - 1)]
    NXC = NT // 512
    xg = [nc.dram_tensor(f"xg{l}", [NXC, 1024, 1024], F32) for l in range(depth - 1)]
    sc = Sched(nc)
    pt = nc.alloc_sbuf_tensor("pid_t", [1, 1], I32)
    tpid = T()
    sc.dma("pool", pt[:], pid, w=[tpid])
    sc._wait("pool", tpid.w)
    reg_p = nc.gpsimd.alloc_register("reg_p")
    reg_t = nc.gpsimd.alloc_register("reg_t")
    nc.gpsimd.reg_load(reg_p, pt[:1, :1])
    nc.gpsimd.reg_mul(reg_p, reg_p, NT)
    for l in range(depth):
        P = L[l]
        if l == 0:
            xin = x_full
        else:
            xin = (lambda j, h=xg[l - 1]: h.ap()[j % NXC, (j // NXC) * 512:(j // NXC + 1) * 512, :])
        xres = x_own if l == 0 else xo[l - 1].ap()
        phase_inproj(nc, sc, S, xin, P["w"], ident, fm, v, z, dt, pfx=f"p1_{l}")
        phase_attn(nc, sc, S, fm, v, ktab, qtab, dtab, ident, P["lam"], P["subln"], P["lin"], mix.ap(), pfx=f"p2_{l}",
                   win_slopes=WIN_SLOPES)
        phase_ssd(nc, sc, S, fm, z, dt, P["convw"], P["convb"], P["dtb"], P["alog"], P["dsk"], P["normw"], tri, ident,
                  rows, mix.ap(), pfx=f"p3_{l}")
        sc.collective("AllGather", [(mix.ap()[k * 128:(k + 1) * 128, :], gath[l].ap()[k]) for k in range(8)], PAIRS)
        tmm = T()
        for gi, (rank, rb) in enumerate(((0, 0), (1, 0), (0, 512), (1, 512))):
            nc.gpsimd.reg_add(reg_t, reg_p, ((rb // 128) * 2 + rank) * 128 * S)
            sc.dma("pool", mixmine.ap()[gi * 512:(gi + 1) * 512, :].rearrange("(k r) t -> k r t", r=128),
                   bass.AP(gath[l], reg_t, [[2 * 128 * S, 4], [S, 128], [1, NT]]), w=[tmm])
        sc.barrier()
        dst = out if l == depth - 1 else xo[l].ap()
        with ExitStack() as es4:
            Wg_ = es4.enter_context(nc.sbuf_tensor(f"pre_Wg{l}", [128, 8, D_FF], BF16))
            Wu_ = es4.enter_context(nc.sbuf_tensor(f"pre_Wu{l}", [128, 8, D_FF], BF16))
            phase_outproj(nc, sc, NT, xres, mixmine.ap(), P["wo"], P["g1"], P["b1"], ident, x1, x1T, pfx=f"p4a_{l}",
                          prefetch=((Wg_, P["wg"]), (Wu_, P["wu"])))
            phase_ffn(nc, sc, NT, x1, x1T, P["wg"], P["wu"], P["wd"], P["g2"], P["b2"], dst, pfx=f"p4b_{l}", pre=(Wg_, Wu_))
        if l < depth - 1:
            sc.collective("AllGather", [(xo[l].ap()[k * 512:(k + 1) * 512, :], xg[l].ap()[k]) for k in range(NXC)], PAIRS)
    return nc


def _c(a):
    return np.ascontiguousarray(a)


def make_in_maps(x, w_in, lambda_q1, lambda_k1, lambda_q2, lambda_k2, subln_w, conv_w, conv_b,
                 dt_bias_fwd, dt_bias_bwd, a_log_fwd, a_log_bwd, d_skip, ssm_norm_w, w_out,
                 ln1_g, ln1_b, w_gate, w_up, w_down, ln2_g, ln2_b, S=SEQ, ncores=NCORES):
    f32 = np.float32
    A = lambda t: np.asarray(t, f32)
    x = A(x)
    NT = S // 2
    ident = np.eye(128, dtype=f32)
    tri = _tri_tables()
    tabs = [_const_tables(S, HEADS_P[p]) for p in range(2)]
    depth = np.asarray(w_in).shape[0]
    worows = np.concatenate([np.arange(h * 128, (h + 1) * 128) for h in HEADS_P[0] + HEADS_P[1]] + [np.arange(1024, 2048)])
    in_maps = []
    for core in range(ncores):
        b, p = core // 2, core % 2
        hcols = np.concatenate([np.arange(h * 128, (h + 1) * 128) for h in HEADS_P[p]])
        cols = np.concatenate([
            hcols,
            1024 + hcols,
            4096 + np.arange(512 * p, 512 * p + 512),
            4096 + 1024 + np.arange(256 * p, 256 * p + 256),
            4096 + 1536 + np.arange(256 * p, 256 * p + 256),
            2048 + hcols,
            3072 + np.arange(512 * p, 512 * p + 512),
            6144 + np.arange(8 * p, 8 * p + 8),
        ])
        cidx = np.concatenate([
            np.arange(512 * p, 512 * p + 512),
            1024 + np.arange(256 * p, 256 * p + 256),
            1536 + np.arange(256 * p, 256 * p + 256),
        ])
        kt, qt, dtb_ = tabs[p]
        hs = slice(8 * p, 8 * p + 8)
        m = dict(x_full=_c(x[b]), x_own=_c(x[b, p * NT:(p + 1) * NT]), pid=np.array([[p]], np.int32),
                 ident=ident, ktab=kt, qtab=qt, dtab=dtb_, tri=tri)
        for l in range(depth):
            lambda_init = 0.8 - 0.6 * math.exp(-0.3 * l)
            m[f"w{l}"] = _c(A(w_in[l])[:, cols])
            m[f"lam{l}"] = _c(np.stack([A(lambda_q1[l]), A(lambda_k1[l]), A(lambda_q2[l]), A(lambda_k2[l])]))
            m[f"subln{l}"] = _c(A(subln_w[l]).reshape(128, 1))
            m[f"lin{l}"] = np.array([[-lambda_init, 1.0 - lambda_init]], f32)
            m[f"convw{l}"] = _c(A(conv_w[l])[:, cidx].T.reshape(8, 128, 5))
            m[f"convb{l}"] = _c(A(conv_b[l])[cidx].reshape(8, 128, 1))
            m[f"dtb{l}"] = _c(np.stack([A(dt_bias_fwd[l])[hs], A(dt_bias_bwd[l])[hs]]))
            m[f"alog{l}"] = _c(np.stack([A(a_log_fwd[l])[hs], A(a_log_bwd[l])[hs]]))
            m[f"dsk{l}"] = _c(A(d_skip[l])[hs].reshape(1, 8))
            m[f"normw{l}"] = _c(A(ssm_norm_w[l])[512 * p:512 * p + 512].reshape(1, 512))
            m[f"wo{l}"] = _c(A(w_out[l])[worows])
            m[f"g1{l}"] = _c(A(ln1_g[l]).reshape(1, 1024))
            m[f"b1{l}"] = _c(A(ln1_b[l]).reshape(1, 1024))
            m[f"g2{l}"] = _c(A(ln2_g[l]).reshape(1, 1024))
            m[f"b2{l}"] = _c(A(ln2_b[l]).reshape(1, 1024))
            m[f"wg{l}"] = _c(A(w_gate[l]))
            m[f"wu{l}"] = _c(A(w_up[l]))
            m[f"wd{l}"] = _c(A(w_down[l]))
        in_maps.append(m)
    return in_maps


def kernel(**inputs):
    in_maps = make_in_maps(**inputs)
    nc = build_fused()
    res = run_bass_kernel_spmd(nc, in_maps, core_ids=list(range(NCORES)))
    outp = np.empty((BATCH, SEQ, 1024), np.float32)
    for core in range(NCORES):
        b, p = core // 2, core % 2
        outp[b, p * HALF:(p + 1) * HALF] = np.asarray(res.results[core]["out"], np.float32)
    return outp
```

```python
import bisect
from contextlib import ExitStack
import numpy as np
import concourse.bass as bass
import concourse.mybir as mybir

F32 = mybir.dt.float32
BF16 = mybir.dt.bfloat16
AF = mybir.ActivationFunctionType
ALU = mybir.AluOpType
AX = mybir.AxisListType

D_MODEL = 1024
D_FF = 2816
LN_EPS = 1e-5
RMS_EPS = 1e-5
ALPHA = 4.0 ** 0.25


class T:
    __slots__ = ("name", "w", "r", "rd")

    def __init__(self, name=""):
        self.name = name
        self.w = None
        self.r = {}
        self.rd = []


class Sched:
    CE = ("pe", "act", "dve", "pool")

    def __init__(self, nc, ndma=16):
        self.nc = nc
        self.eng = {"pe": nc.tensor, "act": nc.scalar, "dve": nc.vector, "pool": nc.gpsimd, "sp": nc.sync}
        self.sem = {e: nc.alloc_semaphore("se_" + e) for e in self.CE}
        self.cnt = {e: 0 for e in self.CE}
        self.ins = {e: [] for e in self.CE}
        self.sig_idx = {e: [] for e in self.CE}
        self.sig_val = {e: [] for e in self.CE}
        self.waited = {e: {} for e in self.eng}
        self.ndma = ndma
        self.dsem = {q: [nc.alloc_semaphore(f"sd_{q}{i}") for i in range(ndma)] for q in ("sp", "pool", "act")}
        self.dn = {q: 0 for q in self.dsem}
        self.keep = [self.sem[e] for e in self.CE]

    def _wait_sem(self, eng, sem, val):
        k = id(sem)
        if self.waited[eng].get(k, 0) >= val:
            return
        self.waited[eng][k] = val
        self.eng[eng].wait_ge(sem, val)

    def _wait(self, eng, ev):
        if ev is None:
            return
        if ev[0] == "c":
            _, e2, idx = ev
            if e2 == "pe" and eng == "pe":
                return
            si = self.sig_idx[e2]
            p = bisect.bisect_left(si, idx)
            if p < len(si):
                val = self.sig_val[e2][p]
            else:
                self.cnt[e2] += 1
                val = self.cnt[e2]
                self.ins[e2][idx].then_inc(self.sem[e2], 1)
                si.append(idx)
                self.sig_val[e2].append(val)
            self._wait_sem(eng, self.sem[e2], val)
        else:
            _, sem, val = ev
            self._wait_sem(eng, sem, val)

    def _deps(self, eng, r, w):
        for t in r:
            self._wait(eng, t.w)
        for t in w:
            self._wait(eng, t.w)
            for ev in t.r.values():
                self._wait(eng, ev)
            for ev in t.rd:
                self._wait(eng, ev)

    def _mark(self, ev, r, w, is_dma):
        for t in r:
            if is_dma:
                t.rd.append(ev)
            else:
                t.r[ev[1]] = ev
        for t in w:
            t.w = ev
            t.r = {}
            t.rd = []

    def c(self, eng, fn, r=(), w=()):
        self._deps(eng, r, w)
        ins = fn(self.eng[eng])
        idx = len(self.ins[eng])
        self.ins[eng].append(ins)
        self._mark(("c", eng, idx), r, w, False)
        return ins

    def dma(self, q, out, in_, r=(), w=()):
        self._deps(q, r, w)
        n = self.dn[q]
        self.dn[q] += 1
        sem = self.dsem[q][n % self.ndma]
        use = n // self.ndma
        if use > 0:
            self._wait_sem(q, sem, 16 * use)
        self.eng[q].dma_start(out=out, in_=in_).then_inc(sem, 16)
        ev = ("d", sem, 16 * (use + 1))
        self._mark(ev, r, w, True)
        return ev

    def barrier(self):
        evs = []
        for e in self.CE:
            if self.ins[e]:
                evs.append(("c", e, len(self.ins[e]) - 1))
        for q in self.dsem:
            n = self.dn[q]
            for i in range(min(n, self.ndma)):
                uses = (n - 1 - i) // self.ndma + 1
                evs.append(("d", self.dsem[q][i], 16 * uses))
        for eng in ("pe", "act", "dve", "pool", "sp"):
            for ev in evs:
                if ev[0] == "c" and ev[1] == eng:
                    continue
                if ev[0] == "c" and ev[1] == "pe" and eng == "pe":
                    continue
                self._wait(eng, ev)
        self.nphase = getattr(self, "nphase", 0) + 1
        for e in self.CE:
            self.sem[e] = self.nc.alloc_semaphore(f"se_{e}_{self.nphase}")
            self.keep.append(self.sem[e])
            self.cnt[e] = 0
            self.ins[e] = []
            self.sig_idx[e] = []
            self.sig_val[e] = []

    def collective(self, kind, pairs, groups):
        if not hasattr(self, "ccsem"):
            self.ccsem = self.nc.alloc_semaphore("cc_sem")
            self.ccn = 0
        for in_ap, out_ap in pairs:
            self.ccn += 1
            self.nc.gpsimd.collective_compute(kind, ALU.bypass, replica_groups=groups, ins=[in_ap], outs=[out_ap]).then_inc(self.ccsem, 1)
        for eng in ("pe", "act", "dve", "pool", "sp"):
            self._wait_sem(eng, self.ccsem, self.ccn)


def _rot(lst, i):
    return lst[i % len(lst)]


class Ctx:
    def __init__(self, nc, es, pfx):
        self.nc, self.es, self.pfx, self.n = nc, es, pfx, 0

    def sb(self, shape, dt, name=None):
        self.n += 1
        h = self.es.enter_context(self.nc.sbuf_tensor(f"{self.pfx}_{name or 's'}{self.n}", list(shape), dt))
        return h

    def ps(self, shape, dt=F32, name=None):
        self.n += 1
        h = self.es.enter_context(self.nc.psum_tensor(f"{self.pfx}_{name or 'p'}{self.n}", list(shape), dt))
        return h


NFM = 16
NCOL = 3080


def phase_inproj(nc, sc, S, x_d, w_d, ident_d, fm_d, v_d, z_d, dt_d, pfx="p1"):
    with ExitStack() as es:
        cx = Ctx(nc, es, pfx)
        Wb = cx.sb([128, 8, NCOL], BF16, "Wb")
        tW = T("Wb")
        ident = cx.sb([128, 128], BF16, "ident")
        tI = T("ident")
        sc.dma("pool", ident[:], ident_d[:, :], w=[tI])
        wv = w_d.rearrange("(c p) n -> p c n", p=128)
        for c in range(8):
            sc.dma("pool", Wb[:, c, :], wv[:, c, :], w=[tW])
        NB = S // 512
        xb = [cx.sb([128, 4, 1024], BF16, "xb") for _ in range(2)]
        txb = [T() for _ in range(2)]
        xT = [cx.sb([128, 8, 512], BF16, "xT") for _ in range(2)]
        txT = [T() for _ in range(2)]
        ptr = [cx.ps([128, 8, 128], BF16, "ptr") for _ in range(2)]
        tptr = [T() for _ in range(2)]
        pp = [cx.ps([128, 512], F32, "pp") for _ in range(4)]
        tpp = [T() for _ in range(4)]
        ob = [cx.sb([128, 512], BF16, "ob") for _ in range(4)]
        tob = [T() for _ in range(4)]
        odt = [cx.sb([128, 8], F32, "odt") for _ in range(2)]
        todt = [T() for _ in range(2)]
        ntr = 0
        npp = 0
        nob = 0
        for j in range(NB):
            xbj, txbj = xb[j % 2], txb[j % 2]
            xTj, txTj = xT[j % 2], txT[j % 2]
            xsrc = x_d(j) if callable(x_d) else x_d[j * 512:(j + 1) * 512, :]
            sc.dma("pool", xbj[:], xsrc.rearrange("(s p) d -> p s d", p=128), w=[txbj])
            for s in range(4):
                p_, tp_ = ptr[ntr % 2], tptr[ntr % 2]
                for c in range(8):
                    sc.c("pe", lambda e, p_=p_, c=c, s=s: e.transpose(p_[:, c, :], xbj[:, s, c * 128:(c + 1) * 128], ident[:]),
                         r=[txbj, tI], w=[tp_])
                dst = xTj[:, :, s * 128:(s + 1) * 128]
                if ntr % 2 == 0:
                    sc.c("act", lambda e, p_=p_, dst=dst: e.activation(dst, p_[:], AF.Copy), r=[tp_], w=[txTj])
                else:
                    sc.c("dve", lambda e, p_=p_, dst=dst: e.tensor_copy(dst, p_[:]), r=[tp_], w=[txTj])
                ntr += 1
            for cc in range(NFM):
                p_, tp_ = pp[npp % 4], tpp[npp % 4]
                npp += 1
                for c in range(8):
                    sc.c("pe", lambda e, p_=p_, c=c, cc=cc: e.matmul(p_[:], Wb[:, c, cc * 128:(cc + 1) * 128], xTj[:, c, :],
                                                                    start=(c == 0), stop=(c == 7)),
                         r=[tW, txTj], w=[tp_])
                o_, to_ = ob[nob % 4], tob[nob % 4]
                scale = 0.125 if cc < 4 else 1.0
                if nob % 2 == 0:
                    sc.c("act", lambda e, p_=p_, o_=o_, scale=scale: e.activation(o_[:], p_[:], AF.Copy, scale=scale),
                         r=[tp_], w=[to_])
                else:
                    sc.c("dve", lambda e, p_=p_, o_=o_, scale=scale: e.tensor_scalar(o_[:], p_[:], scale, None, ALU.mult),
                         r=[tp_], w=[to_])
                nob += 1
                sc.dma("sp", fm_d[cc, :, j * 512:(j + 1) * 512], o_[:], r=[to_])
            for s in range(4):
                for g, (c0, dst) in enumerate(((2048, v_d), (2560, z_d))):
                    p_, tp_ = pp[npp % 4], tpp[npp % 4]
                    npp += 1
                    for c in range(8):
                        sc.c("pe", lambda e, p_=p_, c=c, s=s, c0=c0: e.matmul(p_[:], xTj[:, c, s * 128:(s + 1) * 128],
                                                                            Wb[:, c, c0:c0 + 512], start=(c == 0), stop=(c == 7)),
                             r=[tW, txTj], w=[tp_])
                    o_, to_ = ob[nob % 4], tob[nob % 4]
                    if nob % 2 == 0:
                        sc.c("act", lambda e, p_=p_, o_=o_: e.activation(o_[:], p_[:], AF.Copy), r=[tp_], w=[to_])
                    else:
                        sc.c("dve", lambda e, p_=p_, o_=o_: e.tensor_copy(o_[:], p_[:]), r=[tp_], w=[to_])
                    nob += 1
                    t0 = j * 512 + s * 128
                    sc.dma("sp", dst[t0:t0 + 128, :], o_[:], r=[to_])
                p_, tp_ = pp[npp % 4], tpp[npp % 4]
                npp += 1
                for c in range(8):
                    sc.c("pe", lambda e, p_=p_, c=c, s=s: e.matmul(p_[:, 0:8], xTj[:, c, s * 128:(s + 1) * 128],
                                                                 Wb[:, c, 3072:3080], start=(c == 0), stop=(c == 7)),
                         r=[tW, txTj], w=[tp_])
                o_, to_ = odt[s % 2], todt[s % 2]
                sc.c("dve", lambda e, p_=p_, o_=o_: e.tensor_copy(o_[:], p_[:, 0:8]), r=[tp_], w=[to_])
                t0 = j * 512 + s * 128
                sc.dma("sp", dt_d[t0:t0 + 128, :], o_[:], r=[to_])
        sc.barrier()


def phase_attn(nc, sc, S, fm_d, v_d, ktab_d, qtab_d, dtab_d, ident_d, lam_d, subln_d, lin_d, mix_d, pfx="p2", win_slopes=None, win_th=64.0):
    NKB = S // 128
    NQB = S // 512
    with ExitStack() as es:
        cx = Ctx(nc, es, pfx)
        ident = cx.sb([128, 128], BF16, "ident")
        tI = T()
        sc.dma("pool", ident[:], ident_d[:, :], w=[tI])
        ones = cx.sb([128, 128], BF16, "ones")
        onesf = cx.sb([128, 128], F32, "onesf")
        tC = T()
        sc.c("dve", lambda e: e.memset(ones[:], 1.0), w=[tC])
        sc.c("dve", lambda e: e.memset(onesf[:], 1.0), w=[tC])
        lamt = cx.sb([128, 4, 64], F32, "lamt")
        tl = T()
        for i in range(4):
            sc.dma("sp", lamt[:, i, :], lam_d[i:i + 1, :].partition_broadcast(128), w=[tl])
        lw = cx.sb([128, 8], F32, "lw")
        ljunk = cx.sb([128, 64], F32, "ljunk")
        tlw = T()
        sc.c("dve", lambda e: e.tensor_tensor(ljunk[:], lamt[:, 0, :], lamt[:, 1, :], ALU.mult), r=[tl], w=[tlw])
        sc.c("dve", lambda e: e.reduce_sum(lw[:, 0:1], ljunk[:], axis=AX.X), r=[tlw], w=[tlw])
        sc.c("dve", lambda e: e.tensor_tensor(ljunk[:], lamt[:, 2, :], lamt[:, 3, :], ALU.mult), r=[tl, tlw], w=[tlw])
        sc.c("dve", lambda e: e.reduce_sum(lw[:, 1:2], ljunk[:], axis=AX.X), r=[tlw], w=[tlw])
        sc.c("act", lambda e: e.activation(lw[:, 2:4], lw[:, 0:2], AF.Exp), r=[tlw], w=[tlw])
        sc.c("dve", lambda e: e.tensor_tensor(lw[:, 4:5], lw[:, 3:4], lw[:, 2:3], ALU.subtract), r=[tlw], w=[tlw])
        lin = cx.sb([128, 2], F32, "lin")
        sc.dma("sp", lin[:], lin_d[0:1, :].partition_broadcast(128), w=[tlw])
        sc.c("dve", lambda e: e.tensor_scalar(lw[:, 5:6], lw[:, 4:5], lin[:, 0:1], None, ALU.add), r=[tlw], w=[tlw])
        neglam = lw[:, 5:6]
        sw = cx.sb([128, 2], F32, "sw")
        tsw = T()
        sc.dma("sp", sw[:, 0:1], subln_d[:, :], w=[tsw])
        sc.c("dve", lambda e: e.tensor_scalar(sw[:, 1:2], sw[:, 0:1], lin[:, 1:2], None, ALU.mult), r=[tsw, tlw], w=[tsw])
        wsc = sw[:, 1:2]
        epst = cx.sb([128, 1], F32, "epst")
        sc.c("dve", lambda e: e.memset(epst[:], RMS_EPS), w=[tsw])

        kT = [cx.sb([72, S], BF16, "kT") for _ in range(2)]
        tk = T()
        V = cx.sb([128, NKB, 128], BF16, "V")
        tV = T()
        Dt = cx.sb([128, 128], BF16, "Dt")
        tD = T()
        qL = [[cx.sb([72, 512], BF16, "qL") for _ in range(2)] for _ in range(2)]
        qR = [[cx.sb([72, 512], BF16, "qR") for _ in range(2)] for _ in range(2)]
        tq = [T() for _ in range(2)]
        Sp = [cx.ps([128, 1024], F32, "Sp") for _ in range(2)]
        tSp = [T() for _ in range(2)]
        Op = [cx.ps([128, 512], F32, "Op") for _ in range(2)]
        Lp = [cx.ps([128, 512], F32, "Lp") for _ in range(2)]
        tOL = [T() for _ in range(2)]
        tLp = [T() for _ in range(2)]
        NE = 6
        SPL = 352
        accLs = [cx.sb([128, 1024], F32, "accL") for _ in range(2)]
        taccDs, taccPs = [T(), T()], [T(), T()]
        ocs = [[cx.sb([128, 512], F32, "oc") for _ in range(2)] for _ in range(2)]
        tocs = [T(), T()]
        l1cs = [cx.sb([128, 512], F32, "l1c") for _ in range(2)]
        pending = []
        E = [cx.sb([128, 1024], BF16, "E") for _ in range(NE)]
        tE = [T() for _ in range(NE)]
        rl = cx.sb([128, 512], F32, "rl")
        o0 = cx.sb([128, 512], F32, "o0")
        o1 = cx.sb([128, 512], F32, "o1")
        sq = cx.sb([128, 512], F32, "sq")
        rs = cx.sb([128, 512], F32, "rs")
        tf = T()
        outb = [cx.sb([128, 512], BF16, "outb") for _ in range(2)]
        tout = [T() for _ in range(2)]
        nS = 0
        nE = 0
        items = [(hh, qb) for hh in range(4) for qb in range(NQB)]

        def load_q(i):
            hh, qb = items[i]
            b = i % 2
            cols = slice(qb * 512, (qb + 1) * 512)
            for c in range(2):
                sc.c("pool", lambda e: e.memset(qL[b][c][64:72, :], 0.0), w=[tq[b]])
                sc.c("pool", lambda e: e.memset(qR[b][c][64:72, :], 0.0), w=[tq[b]])
            for c in range(2):
                sc.dma("sp", qL[b][c][0:64, :], fm_d[hh, c * 64:(c + 1) * 64, cols], w=[tq[b]])
                sc.dma("sp", qR[b][c][0:64, :], fm_d[hh, c * 64:(c + 1) * 64, cols], w=[tq[b]])
                sc.dma("pool", qL[b][c][64:68, :], qtab_d[hh, :, cols], w=[tq[b]])
                sc.dma("pool", qR[b][c][68:72, :], qtab_d[hh, :, cols], w=[tq[b]])

        load_q(0)
        for it, (hh, qb) in enumerate(items):
            if qb == 0:
                for c in range(2):
                    sc.dma("sp", kT[c][0:64, :], fm_d[4 + hh, c * 64:(c + 1) * 64, :], w=[tk])
                    sc.dma("pool", kT[c][64:72, :], ktab_d[hh, :, :], w=[tk])
                sc.dma("sp", V[:], v_d[:, hh * 128:(hh + 1) * 128].rearrange("(kb p) d -> p kb d", p=128), w=[tV])
                sc.dma("pool", Dt[:], dtab_d[hh, :, :], w=[tD])
            if it + 1 < len(items):
                load_q(it + 1)
            if True:
                b = it % 2
                cols = slice(qb * 512, (qb + 1) * 512)
                def emitS(kb):
                    ks = slice(kb * 128, (kb + 1) * 128)
                    sp_, tsp_ = Sp[kb % 2], tSp[kb % 2]
                    for c in range(2):
                        co = c * 512
                        if kb < 4 * qb:
                            sc.c("pe", lambda e: e.matmul(sp_[:, co:co + 512], kT[c][0:72, ks], qL[b][c][0:72, :], start=True, stop=True),
                                 r=[tk, tq[b]], w=[tsp_])
                        elif kb > 4 * qb + 3:
                            sc.c("pe", lambda e: e.matmul(sp_[:, co:co + 512], kT[c][0:72, ks], qR[b][c][0:72, :], start=True, stop=True),
                                 r=[tk, tq[b]], w=[tsp_])
                        else:
                            t = kb - 4 * qb
                            for u in range(4):
                                us = slice(u * 128, (u + 1) * 128)
                                ps_ = slice(co + u * 128, co + (u + 1) * 128)
                                if u < t:
                                    sc.c("pe", lambda e: e.matmul(sp_[:, ps_], kT[c][0:72, ks], qR[b][c][0:72, us], start=True, stop=True),
                                         r=[tk, tq[b]], w=[tsp_])
                                elif u > t:
                                    sc.c("pe", lambda e: e.matmul(sp_[:, ps_], kT[c][0:72, ks], qL[b][c][0:72, us], start=True, stop=True),
                                         r=[tk, tq[b]], w=[tsp_])
                                else:
                                    sc.c("pe", lambda e: e.matmul(sp_[:, ps_], kT[c][0:64, ks], qL[b][c][0:64, us], start=True, stop=False),
                                         r=[tk, tq[b]], w=[tsp_])
                                    sc.c("pe", lambda e: e.matmul(sp_[:, ps_], ident[:], Dt[:], start=False, stop=True),
                                         r=[tI, tD], w=[tsp_])

                def emitRest(kb):
                    nonlocal nE
                    sp_, tsp_ = Sp[kb % 2], tSp[kb % 2]
                    e_, te_ = E[nE % NE], tE[nE % NE]
                    nE += 1
                    sc.c("act", lambda e: e.activation(e_[:], sp_[:], AF.Exp), r=[tsp_], w=[te_])
                    for c in range(2):
                        es = slice(c * 512, (c + 1) * 512)
                        sc.c("pe", lambda e: e.matmul(Op[c][:], V[:, kb, :], e_[:, es], start=(kb == KB0), stop=(kb == KB1)),
                             r=[tV, te_], w=[tOL[c]])
                    sc.c("pe", lambda e: e.matmul(Lp[1][:], ones[:], e_[:, 512:1024], start=(kb == KB0), stop=(kb == KB1)),
                         r=[tC, te_], w=[tLp[1]])
                    if kb == KB0:
                        sc.c("dve", lambda e: e.tensor_copy(accL[:, 0:SPL], e_[:, 0:SPL]), r=[te_], w=[taccD])
                        sc.c("pool", lambda e: e.tensor_copy(accL[:, SPL:512], e_[:, SPL:512]), r=[te_], w=[taccP])
                    else:
                        sc.c("dve", lambda e: e.tensor_tensor(accL[:, 0:SPL], accL[:, 0:SPL], e_[:, 0:SPL], ALU.add), r=[te_, taccD], w=[taccD])
                        sc.c("pool", lambda e: e.tensor_tensor(accL[:, SPL:512], accL[:, SPL:512], e_[:, SPL:512], ALU.add), r=[te_, taccP], w=[taccP])

                par = it % 2
                accL, taccD, taccP = accLs[par], taccDs[par], taccPs[par]
                if win_slopes is None:
                    kbs = list(range(NKB))
                else:
                    m_ = win_slopes[hh]
                    i0, i1 = qb * 512, qb * 512 + 511
                    kbs = [k_ for k_ in range(NKB) if m_ * max(0, k_ * 128 - i1, i0 - (k_ * 128 + 127)) <= win_th]
                KB0, KB1 = kbs[0], kbs[-1]
                emitS(kbs[0])
                step = max(1, (len(kbs) - 4) // 14)
                for idx, kb in enumerate(kbs):
                    if idx + 1 < len(kbs):
                        emitS(kbs[idx + 1])
                    emitRest(kb)
                    if pending and idx >= 2 and (idx - 2) % step == 0:
                        pending.pop(0)()
                while pending:
                    pending.pop(0)()
                oc, toc = ocs[par], tocs[par]
                for c in range(2):
                    sc.c("dve", lambda e: e.tensor_copy(oc[c][:], Op[c][:]), r=[tOL[c]], w=[toc])
                l1c = l1cs[par]
                sc.c("dve", lambda e: e.tensor_copy(l1c[:], Lp[1][:]), r=[tLp[1]], w=[toc])

                def mk_final(hh=hh, qb=qb, cols=cols, accL=accL, taccD=taccD, taccP=taccP, oc=oc, toc=toc, l1c=l1c):
                    ops = []
                    ops.append(lambda: sc.c("pe", lambda e: e.matmul(Lp[0][:], onesf[:], accL[:, 0:512], start=True, stop=True),
                                            r=[tC, taccD, taccP], w=[tLp[0]]))
                    ops.append(lambda: sc.c("dve", lambda e: e.reciprocal(rl[:], Lp[0][:]), r=[tLp[0]], w=[tf]))
                    ops.append(lambda: sc.c("dve", lambda e: e.tensor_tensor(o0[:], oc[0][:], rl[:], ALU.mult), r=[toc, tf], w=[tf]))
                    ops.append(lambda: sc.c("dve", lambda e: e.reciprocal(rl[:], l1c[:]), r=[toc, tf], w=[tf]))
                    ops.append(lambda: sc.c("dve", lambda e: e.tensor_tensor(o1[:], oc[1][:], rl[:], ALU.mult), r=[toc, tf], w=[tf]))
                    ops.append(lambda: sc.c("dve", lambda e: e.scalar_tensor_tensor(o0[:], o1[:], neglam, o0[:], ALU.mult, ALU.add), r=[tf, tlw], w=[tf]))
                    ops.append(lambda: sc.c("act", lambda e: e.activation(sq[:], o0[:], AF.Square), r=[tf], w=[tf]))
                    ops.append(lambda: sc.c("pe", lambda e: e.matmul(Lp[0][:], onesf[:], sq[:], start=True, stop=True), r=[tC, tf], w=[tLp[0]]))
                    ops.append(lambda: sc.c("act", lambda e: e.activation(rs[:], Lp[0][:], AF.Ln, bias=epst[:, 0:1], scale=1.0 / 128.0), r=[tLp[0], tsw], w=[tf]))
                    ops.append(lambda: sc.c("act", lambda e: e.activation(rs[:], rs[:], AF.Exp, scale=-0.5), r=[tf], w=[tf]))
                    ops.append(lambda: sc.c("dve", lambda e: e.tensor_tensor(o0[:], o0[:], rs[:], ALU.mult), r=[tf], w=[tf]))

                    def last():
                        ob_, tob_ = outb[qb % 2], tout[qb % 2]
                        sc.c("act", lambda e: e.activation(ob_[:], o0[:], AF.Copy, scale=wsc), r=[tf, tsw], w=[tob_])
                        sc.dma("sp", mix_d[hh * 128:(hh + 1) * 128, cols], ob_[:], r=[tob_])
                    ops.append(last)
                    return ops
                pending.extend(mk_final())
        while pending:
            pending.pop(0)()
        sc.barrier()


def phase_ssd(nc, sc, S, fm_d, z_d, dt_d, convw_d, convb_d, dtb_d, alog_d, dsk_d, normw_d, tri_d, ident_d,
              rows_d, mix_d, pfx="p3"):
    NCH = S // 128
    TP = min(512, S)
    with ExitStack() as es:
        cx = Ctx(nc, es, pfx)
        ident = cx.sb([128, 128], BF16, "ident")
        tri = cx.sb([128, 4, 128], F32, "tri")
        onesf = cx.sb([128, 128], F32, "onesf")
        tC = T()
        sc.dma("pool", ident[:], ident_d[:, :], w=[tC])
        for i in range(4):
            sc.dma("sp", tri[:, i, :], tri_d[i, :, :], w=[tC])
        sc.c("dve", lambda e: e.memset(onesf[:], 1.0), w=[tC])
        LE, LT, GE, GT = (tri[:, i, :] for i in range(4))
        cst = cx.sb([128, 2], F32, "cst")
        tcs = T()
        sc.c("dve", lambda e: e.memset(cst[:, 0:1], 1.0), w=[tcs])
        sc.c("dve", lambda e: e.memset(cst[:, 1:2], RMS_EPS), w=[tcs])
        one_c, eps_c = cst[:, 0:1], cst[:, 1:2]
        BTc = cx.sb([128, S], BF16, "BTc")
        CTc = cx.sb([128, S], BF16, "CTc")
        xtok = cx.sb([128, NCH, 256], BF16, "xtok")
        Btok = cx.sb([128, NCH, 128], BF16, "Btok")
        yf = cx.sb([128, NCH, 256], F32, "yf")
        tB, tCc, txt, tBt, tyf = T(), T(), T(), T(), T()
        raw = [cx.sb([128, TP + 4], BF16, "raw") for _ in range(2)]
        traw = [T(), T()]
        acc = [cx.sb([128, TP], F32, "acc") for _ in range(2)]
        tacc = [T(), T()]
        xsT = cx.sb([128, TP], BF16, "xsT")
        txs = T()
        cw = cx.sb([128, 8, 6], F32, "cw")
        tcw = T()
        for i in range(8):
            sc.dma("sp", cw[:, i, 0:5], convw_d[i, :, :], w=[tcw])
            sc.dma("sp", cw[:, i, 5:6], convb_d[i, :, :], w=[tcw])
        ptr = [cx.ps([128, 4, 128], BF16, "ptr") for _ in range(2)]
        tptr = [T(), T()]
        pst = [cx.ps([128, 256], F32, "pst") for _ in range(2)]
        tpst = [T(), T()]
        pcb = cx.ps([128, 256], F32, "pcb")
        tpcb = [T(), T()]
        pY = cx.ps([128, 256], F32, "pY")
        tpY = T()
        pYo = cx.ps([128, 256], F32, "pYo")
        tpYo = T()
        pS = cx.ps([128, 256], F32, "pS")
        tpS = T()
        dtr = cx.sb([128, NCH, 4], F32, "dtr")
        dts = cx.sb([128, NCH, 4], F32, "dts")
        dA = cx.sb([128, NCH, 4], F32, "dA")
        fac2 = cx.sb([128, NCH, 4], F32, "fac2")
        eA = cx.sb([128, NCH * 4], F32, "eA")
        eD = cx.sb([128, NCH * 4], F32, "eD")
        eT = cx.sb([128, NCH * 4], F32, "eT")
        sclr = cx.sb([128, NCH * 4], F32, "sclr")
        rowsb = cx.sb([128, 2, 128], F32, "rowsb")
        par = cx.sb([128, 5, 8], F32, "par")
        nw = cx.sb([128, 512], F32, "nw")
        tdt, tst, tpar = T(), T(), T()
        for i in range(2):
            sc.dma("sp", par[:, i, :], dtb_d[i:i + 1, :].partition_broadcast(128), w=[tpar])
            sc.dma("sp", par[:, 2 + i, :], alog_d[i:i + 1, :].partition_broadcast(128), w=[tpar])
        sc.dma("sp", par[:, 4, :], dsk_d[0:1, :].partition_broadcast(128), w=[tpar])
        sc.dma("sp", nw[:], normw_d[0:1, :].partition_broadcast(128), w=[tpar])
        sc.c("act", lambda e: e.activation(par[:, 2:4, :], par[:, 2:4, :], AF.Exp), r=[tpar], w=[tpar])
        sc.c("dve", lambda e: e.tensor_scalar(par[:, 2:4, :], par[:, 2:4, :], -1.0, None, ALU.mult), r=[tpar], w=[tpar])
        Rb = [cx.sb([128, 4, 128], F32, "Rb") for _ in range(3)]
        tRb = [T(), T(), T()]
        Ex = [cx.sb([128, 4, 128], F32, "Ex") for _ in range(2)]
        tEx = [T(), T()]
        cbm = [cx.sb([128, 128], F32, "cbm") for _ in range(2)]
        tcbm = [T(), T()]
        MT = [cx.sb([128, 4, 128], BF16, "MT") for _ in range(2)]
        tMT = [T(), T()]
        xdt = [cx.sb([128, 256], BF16, "xdt") for _ in range(2)]
        xdec = [cx.sb([128, 256], BF16, "xdec") for _ in range(2)]
        txd = [T(), T()]
        Ysb = cx.sb([128, 256], F32, "Ysb")
        tYs = T()
        yb = cx.sb([128, 256], F32, "yb")
        tyb = T()
        ST = cx.sb([128, 256], F32, "ST")
        STb = cx.sb([128, 256], BF16, "STb")
        tST, tSTb = T(), T()
        tz = [T(), T()]
        zbig = [cx.sb([128, 8, 256], BF16, "zbig") for _ in range(2)]
        tzbig = [T(), T()]
        dx4s = [cx.sb([128, 4, 256], F32, "dx4") for _ in range(2)]
        jk4s = [cx.sb([128, 4, 256], BF16, "jk4")] * 2
        ob4s = [cx.sb([128, 4, 256], BF16, "ob4")] * 2
        fs4s = [cx.sb([128, 12], F32, "fs4") for _ in range(2)]
        tdx4s, tjk4s, tob4s, tfs4s = [T(), T()], [T()] * 2, [T()] * 2, [T(), T()]
        tfin = T()
        oT = [cx.sb([128, 2, 512], BF16, "oT") for _ in range(2)]
        toT = [T(), T()]
        trows = T()
        trowd = [T(), T()]
        nraw = 0
        ntr = 0
        for g in range(2):
            for kind, cid in (("B", 12 + g), ("C", 14 + g), ("x0", 8 + 2 * g), ("x1", 9 + 2 * g)):
                for pc in range(S // TP):
                    t0 = pc * TP
                    rw, trw = raw[nraw % 2], traw[nraw % 2]
                    ac, tac = acc[nraw % 2], tacc[nraw % 2]
                    nraw += 1
                    lo = max(t0 - 2, 0)
                    hi = min(t0 + TP + 2, S)
                    if t0 == 0:
                        sc.c("pool", lambda e: e.memset(rw[:, 0:2], 0.0), w=[trw])
                    if t0 + TP == S:
                        sc.c("pool", lambda e: e.memset(rw[:, TP + 2:TP + 4], 0.0), w=[trw])
                    sc.dma("sp", rw[:, lo - (t0 - 2):hi - (t0 - 2)], fm_d[cid, :, lo:hi], w=[trw])
                    sc.c("dve", lambda e: e.tensor_scalar(ac[:], rw[:, 0:TP], cw[:, cid - 8, 0:1], None, ALU.mult), r=[trw, tcw], w=[tac])
                    for w_ in range(1, 5):
                        sc.c("dve", lambda e: e.scalar_tensor_tensor(ac[:], rw[:, w_:w_ + TP], cw[:, cid - 8, w_:w_ + 1], ac[:], ALU.mult, ALU.add),
                             r=[trw, tcw, tac], w=[tac])
                    if kind == "B":
                        sc.c("act", lambda e: e.activation(BTc[:, t0:t0 + TP], ac[:], AF.Silu, bias=cw[:, cid - 8, 5:6]), r=[tac, tcw], w=[tB])
                        src, tsrc = BTc, tB
                    elif kind == "C":
                        sc.c("act", lambda e: e.activation(CTc[:, t0:t0 + TP], ac[:], AF.Silu, bias=cw[:, cid - 8, 5:6]), r=[tac, tcw], w=[tCc])
                        continue
                    else:
                        sc.c("act", lambda e: e.activation(xsT[:], ac[:], AF.Silu, bias=cw[:, cid - 8, 5:6]), r=[tac, tcw], w=[txs])
                    for q4 in range(TP // 512):
                        p_, tp_ = ptr[ntr % 2], tptr[ntr % 2]
                        ntr += 1
                        for u in range(4):
                            c0 = q4 * 512 + u * 128
                            if kind == "B":
                                sc.c("pe", lambda e: e.transpose(p_[:, u, :], BTc[:, t0 + c0:t0 + c0 + 128], ident[:]), r=[tB, tC], w=[tp_])
                            else:
                                sc.c("pe", lambda e: e.transpose(p_[:, u, :], xsT[:, c0:c0 + 128], ident[:]), r=[txs, tC], w=[tp_])
                        ch0 = (t0 + q4 * 512) // 128
                        if kind == "B":
                            sc.c("act", lambda e: e.activation(Btok[:, ch0:ch0 + 4, :], p_[:], AF.Copy), r=[tp_], w=[tBt])
                        else:
                            half = 0 if kind == "x0" else 1
                            sc.c("act", lambda e: e.activation(xtok[:, ch0:ch0 + 4, half * 128:(half + 1) * 128], p_[:], AF.Copy), r=[tp_], w=[txt])
            tzd = []
            ZP = min(8, NCH)
            for zi in range(NCH // ZP):
                zb_, tzb_ = zbig[zi % 2], tzbig[zi % 2]
                zsrc = z_d[zi * ZP * 128:(zi + 1) * ZP * 128, g * 256:(g + 1) * 256].rearrange("(c l) d -> l c d", l=128)
                sc.dma("sp", zb_[:, 0:ZP, :], zsrc, w=[tzb_])
                sc.c("act", lambda e: e.activation(zb_[:, 0:ZP, :], zb_[:, 0:ZP, :], AF.Silu), r=[tzb_], w=[tzb_])
                tz1 = T()
                sc.dma("sp", zsrc, zb_[:, 0:ZP, :], r=[tzb_], w=[tz1])
                tzd.append(tz1)
            sc.dma("sp", dtr[:], dt_d[:, g * 4:(g + 1) * 4].rearrange("(c l) r -> l c r", l=128), w=[tdt])
            for d in range(2):
                fwd = (d == 0)
                for r_ in range(4):
                    sc.c("dve", lambda e: e.tensor_scalar(dts[:, :, r_], dtr[:, :, r_], par[:, d, g * 4 + r_:g * 4 + r_ + 1], None, ALU.add),
                         r=[tdt, tpar, tst], w=[tst])
                sc.c("act", lambda e: e.activation(dts[:], dts[:], AF.Exp), r=[tst], w=[tst])
                sc.c("act", lambda e: e.activation(dts[:], dts[:], AF.Ln, bias=one_c), r=[tst, tcs], w=[tst])
                for r_ in range(4):
                    sc.c("dve", lambda e: e.tensor_scalar(dA[:, :, r_], dts[:, :, r_], par[:, 2 + d, g * 4 + r_:g * 4 + r_ + 1], None, ALU.mult),
                         r=[tst, tpar], w=[tst])
                dAf = dA[:].rearrange("p c r -> p (c r)")
                M1 = LE if fwd else GE
                M2 = GT if fwd else LT
                MR = LE if fwd else LT
                for (mat, dst, keep) in ((M1, eA, fwd), (M2, eD, not fwd), (onesf[:], eT, False)):
                    for hf in range(NCH * 4 // 256 if NCH * 4 >= 256 else 1):
                        wd_ = min(256, NCH * 4)
                        p_, tp_ = pst[hf % 2], tpst[hf % 2]
                        sc.c("pe", lambda e: e.matmul(p_[:, 0:wd_], mat, dAf[:, hf * wd_:(hf + 1) * wd_], start=True, stop=True), r=[tC, tst], w=[tp_])
                        if keep:
                            sc.c("dve", lambda e: e.tensor_copy(sclr[:, hf * wd_:(hf + 1) * wd_], p_[:, 0:wd_]), r=[tp_], w=[tst])
                        sc.c("act", lambda e: e.activation(dst[:, hf * wd_:(hf + 1) * wd_], p_[:, 0:wd_], AF.Exp), r=[tp_], w=[tst])
                sc.c("dve", lambda e: e.tensor_tensor(fac2[:].rearrange("p c r -> p (c r)"), dts[:].rearrange("p c r -> p (c r)"), eD[:], ALU.mult), r=[tst], w=[tst])
                nrow = NCH * 4
                for hf in range((nrow + 127) // 128):
                    m_ = min(128, nrow - hf * 128)
                    p_, tp_ = pst[hf % 2], tpst[hf % 2]
                    sc.c("pe", lambda e: e.matmul(p_[0:m_, 0:128], dAf[:, hf * 128:hf * 128 + m_], MR, start=True, stop=True), r=[tC, tst], w=[tp_])
                    sc.c("dve", lambda e: e.tensor_copy(rowsb[0:m_, hf, :], p_[0:m_, 0:128]), r=[tp_], w=[trows])
                    sc.dma("sp", rows_d[d].rearrange("c (r l) -> (c r) l", l=128)[hf * 128:hf * 128 + m_, :], rowsb[0:m_, hf, :], r=[trows], w=[trowd[hf]])
                sc.c("dve", lambda e: e.memset(ST[:], 0.0), w=[tST])
                sc.c("dve", lambda e: e.memset(STb[:], 0.0), w=[tSTb])
                order = list(range(NCH)) if fwd else list(range(NCH - 1, -1, -1))
                v3 = lambda ap: ap.rearrange("p (r d) -> p r d", d=64)
                bc = lambda ap, n: ap.unsqueeze(2).to_broadcast([128, 4, n])

                def rb_load(n_):
                    ci = order[n_]
                    rb, trb = Rb[n_ % 3], tRb[n_ % 3]
                    sc.dma("sp", rb[:].rearrange("p r l -> p (r l)"), rows_d[d, ci:ci + 1, :].partition_broadcast(128),
                           r=trowd, w=[trb])

                def front(n_):
                    ci = order[n_]
                    k_ = n_ % 2
                    cs = slice(ci * 128, (ci + 1) * 128)
                    c4 = slice(ci * 4, (ci + 1) * 4)
                    rb, trb = Rb[n_ % 3], tRb[n_ % 3]
                    sc.c("dve", lambda e: e.tensor_tensor(rb[:], rb[:], bc(sclr[:, c4], 128), ALU.subtract), r=[tst], w=[trb])
                    sc.c("act", lambda e: e.activation(Ex[k_][:], rb[:], AF.Exp, scale=1.0 if fwd else -1.0), r=[trb], w=[tEx[k_]])
                    sc.c("pe", lambda e: e.matmul(pcb[:, k_ * 128:(k_ + 1) * 128], BTc[:, cs], CTc[:, cs], start=True, stop=True), r=[tB, tCc], w=[tpcb[k_]])
                    sc.c("dve", lambda e: e.tensor_tensor(cbm[k_][:], pcb[:, k_ * 128:(k_ + 1) * 128], LE if fwd else GE, ALU.mult), r=[tpcb[k_], tC], w=[tcbm[k_]])
                    sc.c("dve", lambda e: e.scalar_tensor_tensor(MT[k_][:], Ex[k_][:], 1.0, cbm[k_][:].unsqueeze(1).to_broadcast([128, 4, 128]), ALU.min, ALU.mult),
                         r=[tEx[k_], tcbm[k_]], w=[tMT[k_]])
                    sc.c("pool", lambda e: e.tensor_tensor(v3(xdt[k_][:]), v3(xtok[:, ci, :]), bc(dts[:, ci, :], 64), ALU.mult), r=[txt, tst], w=[txd[k_]])
                    sc.c("pool", lambda e: e.tensor_tensor(v3(xdec[k_][:]), v3(xtok[:, ci, :]), bc(fac2[:, ci, :], 64), ALU.mult), r=[txt, tst], w=[txd[k_]])

                rb_load(0)
                if NCH > 1:
                    rb_load(1)
                front(0)
                for n_, ci in enumerate(order):
                    if n_ + 2 < NCH:
                        rb_load(n_ + 2)
                    if n_ + 1 < NCH:
                        front(n_ + 1)
                    k_ = n_ % 2
                    cs = slice(ci * 128, (ci + 1) * 128)
                    c4 = slice(ci * 4, (ci + 1) * 4)
                    for r_ in range(4):
                        rs_ = slice(r_ * 64, (r_ + 1) * 64)
                        sc.c("pe", lambda e: e.matmul(pY[:, rs_], MT[k_][:, r_, :], xdt[k_][:, rs_], start=True, stop=True), r=[tMT[k_], txd[k_]], w=[tpY])
                    sc.c("pe", lambda e: e.matmul(pYo[:], CTc[:, cs], STb[:], start=True, stop=True), r=[tCc, tSTb], w=[tpYo])
                    ydst, tyd = (yf[:, ci, :], tyf) if fwd else (yb[:], tyb)
                    sc.c("dve", lambda e: e.tensor_tensor(v3(Ysb[:]), v3(pYo[:]), bc(eA[:, c4], 64), ALU.mult), r=[tpYo, tst], w=[tYs])
                    sc.c("dve", lambda e: e.tensor_tensor(ydst, Ysb[:], pY[:], ALU.add), r=[tYs, tpY], w=[tyd])
                    sc.c("pe", lambda e: e.matmul(pS[:], Btok[:, ci, :], xdec[k_][:], start=True, stop=True), r=[tBt, txd[k_]], w=[tpS])
                    sc.c("dve", lambda e: e.tensor_tensor(v3(ST[:]), v3(ST[:]), bc(eT[:, c4], 64), ALU.mult), r=[tst, tST], w=[tST])
                    sc.c("dve", lambda e: e.tensor_tensor(ST[:], ST[:], pS[:], ALU.add), r=[tpS, tST], w=[tST])
                    sc.c("act", lambda e: e.activation(STb[:], ST[:], AF.Copy), r=[tST], w=[tSTb])
                    if not fwd:
                        sc.c("dve", lambda e: e.tensor_tensor(yf[:, ci, :], yf[:, ci, :], yb[:], ALU.add), r=[tyb, tyf], w=[tyf])
            v4 = lambda ap: ap.rearrange("p c (r d) -> p c r d", d=64)
            for blk in range(NCH // 4 if NCH >= 4 else 1):
                nb_ = min(4, NCH)
                c0_ = blk * nb_
                zb_, tzb_ = zbig[blk % 2], tzbig[blk % 2]
                dx4, jk4, ob4, fs4 = dx4s[blk % 2], jk4s[blk % 2], ob4s[blk % 2], fs4s[blk % 2]
                tdx4, tjk4, tob4, tfs4 = tdx4s[blk % 2], tjk4s[blk % 2], tob4s[blk % 2], tfs4s[blk % 2]
                sc.dma("sp", zb_[:, 0:nb_, :], z_d[c0_ * 128:(c0_ + nb_) * 128, g * 256:(g + 1) * 256].rearrange("(c l) d -> l c d", l=128),
                       r=tzd, w=[tzb_])
                yv = yf[:, c0_:c0_ + nb_, :]
                dsk_b = par[:, 4, g * 4:(g + 1) * 4].unsqueeze(1).unsqueeze(3).to_broadcast([128, nb_, 4, 64])
                sc.c("pool", lambda e: e.tensor_tensor(v4(dx4[:, 0:nb_, :]), v4(xtok[:, c0_:c0_ + nb_, :]), dsk_b, ALU.mult), r=[txt, tpar], w=[tdx4])
                sc.c("dve", lambda e: e.tensor_tensor(yv, yv, dx4[:, 0:nb_, :], ALU.add), r=[tdx4, tyf], w=[tyf])
                sc.c("dve", lambda e: e.tensor_tensor(dx4[:, 0:nb_, :], yv, zb_[:, 0:nb_, :], ALU.mult), r=[tyf, tzb_, tdx4], w=[tdx4])
                sc.c("act", lambda e: e.activation(jk4[:, 0:nb_, :], dx4[:, 0:nb_, :], AF.Square), r=[tdx4], w=[tjk4])
                sc.c("dve", lambda e: e.reduce_sum(fs4[:, 0:nb_], jk4[:, 0:nb_, :], axis=AX.X), r=[tjk4], w=[tfs4])
                sc.c("act", lambda e: e.activation(fs4[:, 4:4 + nb_], fs4[:, 0:nb_], AF.Ln, bias=eps_c, scale=1.0 / 256.0), r=[tfs4, tcs], w=[tfs4])
                sc.c("act", lambda e: e.activation(fs4[:, 8:8 + nb_], fs4[:, 4:4 + nb_], AF.Exp, scale=-0.5), r=[tfs4], w=[tfs4])
                sc.c("dve", lambda e: e.tensor_tensor(dx4[:, 0:nb_, :], dx4[:, 0:nb_, :], fs4[:, 8:8 + nb_].unsqueeze(2).to_broadcast([128, nb_, 256]), ALU.mult),
                     r=[tfs4, tdx4], w=[tdx4])
                sc.c("dve", lambda e: e.tensor_tensor(ob4[:, 0:nb_, :], dx4[:, 0:nb_, :], nw[:, g * 256:(g + 1) * 256].unsqueeze(1).to_broadcast([128, nb_, 256]), ALU.mult),
                     r=[tdx4, tpar], w=[tob4])
                o_, to_ = oT[blk % 2], toT[blk % 2]
                for hf in range(2):
                    p_, tp_ = ptr[hf], tptr[hf]
                    for u in range(nb_):
                        sc.c("pe", lambda e: e.transpose(p_[:, u, :], ob4[:, u, hf * 128:(hf + 1) * 128], ident[:]), r=[tob4, tC], w=[tp_])
                    sc.c("act", lambda e: e.activation(o_[:, hf, 0:nb_ * 128].rearrange("p (u t) -> p u t", t=128), p_[:, 0:nb_, :], AF.Copy), r=[tp_], w=[to_])
                for hf in range(2):
                    r0 = 512 + g * 256 + hf * 128
                    sc.dma("sp", mix_d[r0:r0 + 128, c0_ * 128:(c0_ + nb_) * 128], o_[:, hf, 0:nb_ * 128], r=[to_])
        sc.barrier()


def g_off(x):
    return 0


def _layernorm(sc, cx_tiles, r, tr, g, b, tgb, out, tout):
    st, junk, tst = cx_tiles
    sc.c("dve", lambda e: e.reduce_sum(st[:, 0:1], r[:], axis=AX.X), r=[tr], w=[tst])
    sc.c("dve", lambda e: e.tensor_scalar(st[:, 1:2], st[:, 0:1], -1.0 / 1024.0, None, ALU.mult), r=[tst], w=[tst])
    sc.c("dve", lambda e: e.memset(st[:, 2:3], 0.0), w=[tst])
    sc.c("act", lambda e: e.activation(junk[:], r[:], AF.Square, bias=st[:, 1:2], accum_out=st[:, 2:3]), r=[tr, tst], w=[tst])
    sc.c("act", lambda e: e.activation(st[:, 3:4], st[:, 2:3], AF.Sqrt, bias=st[:, 5:6], scale=1.0 / 1024.0), r=[tst], w=[tst])
    sc.c("dve", lambda e: e.reciprocal(st[:, 4:5], st[:, 3:4]), r=[tst], w=[tst])
    sc.c("dve", lambda e: e.tensor_scalar(r[:], r[:], st[:, 1:2], st[:, 4:5], ALU.add, ALU.mult), r=[tst, tr], w=[tr])
    sc.c("dve", lambda e: e.tensor_tensor(r[:], r[:], g[:], ALU.mult), r=[tr, tgb], w=[tr])
    sc.c("dve", lambda e: e.tensor_tensor(out[:], r[:], b[:], ALU.add), r=[tr, tgb], w=[tout])


def phase_outproj(nc, sc, NT, x_d, mixT_d, wo_d, g_d, b_d, ident_d, x1_d, x1T_d, pfx="p4a", dyn=None, prefetch=None):
    with ExitStack() as es:
        cx = Ctx(nc, es, pfx)
        Wo = cx.sb([128, 16, 1024], BF16, "Wo")
        tW = T()
        wv = wo_d.rearrange("(c p) n -> p c n", p=128)
        for c in range(16):
            sc.dma("pool", Wo[:, c, :], wv[:, c, :], w=[tW])
        ident = cx.sb([128, 128], BF16, "ident")
        sc.dma("pool", ident[:], ident_d[:, :], w=[tW])
        if prefetch is not None:
            tpf = T()
            for (Wt, wd_) in prefetch:
                wv2 = wd_.rearrange("(c p) n -> p c n", p=128)
                for c in range(8):
                    sc.dma("pool", Wt[:, c, :], wv2[:, c, :], w=[tpf])
        g = cx.sb([128, 1024], F32, "g")
        b = cx.sb([128, 1024], F32, "b")
        tgb = T()
        sc.dma("sp", g[:], g_d[0:1, :].partition_broadcast(128), w=[tgb])
        sc.dma("sp", b[:], b_d[0:1, :].partition_broadcast(128), w=[tgb])
        st = cx.sb([128, 8], F32, "st")
        junk = cx.sb([128, 1024], F32, "junk")
        tst = T()
        sc.c("dve", lambda e: e.memset(st[:, 5:6], LN_EPS), w=[tst])
        mixT = [cx.sb([128, 16, 512], BF16, "mixT") for _ in range(2)]
        tmx = [T(), T()]
        xres = [cx.sb([128, 1024], F32, "xres") for _ in range(2)]
        txr = [T(), T()]
        rt = [cx.sb([128, 1024], F32, "rt") for _ in range(2)]
        trt = [T(), T()]
        x1 = [cx.sb([128, 1024], F32, "x1") for _ in range(2)]
        tx1 = [T(), T()]
        x1b = cx.sb([128, 1024], BF16, "x1b")
        tx1b = T()
        x1T = [cx.sb([128, 8, 512], BF16, "x1T")] * 2
        tx1T = [T()] * 2
        po = [cx.ps([128, 1024], F32, "po") for _ in range(2)]
        tpo = [T(), T()]
        ptr = [cx.ps([128, 8, 128], BF16, "ptr") for _ in range(2)]
        tptr = [T(), T()]
        n = 0
        NB = NT // 512
        if dyn is None:
            mv = mixT_d.rearrange("(c p) t -> p c t", p=128)

            def load_mix(blk_, dst, tdst):
                sc.dma("sp", dst[:], mv[:, :, blk_ * 512:(blk_ + 1) * 512], w=[tdst])
        else:
            gath_h, SG, reg_p, reg_t = dyn

            def load_mix(blk_, dst, tdst):
                for gi, (rank, rb) in enumerate(((0, 0), (1, 0), (0, 512), (1, 512))):
                    off = (rank * 1024 + rb) * SG + blk_ * 512
                    nc.gpsimd.reg_add(reg_t, reg_p, off)
                    sc.dma("pool", dst[:, gi * 4:(gi + 1) * 4, :], bass.AP(gath_h, reg_t, [[SG, 128], [128 * SG, 4], [1, 512]]), w=[tdst])
        load_mix(0, mixT[0], tmx[0])
        NTILE = NB * 4

        def mm(n_):
            blk_, s_ = divmod(n_, 4)
            mx, tm = mixT[blk_ % 2], tmx[blk_ % 2]
            p_, tp_ = po[n_ % 2], tpo[n_ % 2]
            for hf in range(2):
                for c in range(16):
                    sc.c("pe", lambda e: e.matmul(p_[:, hf * 512:(hf + 1) * 512], mx[:, c, s_ * 128:(s_ + 1) * 128], Wo[:, c, hf * 512:(hf + 1) * 512],
                                                 start=(c == 0), stop=(c == 15)), r=[tm, tW], w=[tp_])

        mm(0)
        for n in range(NTILE):
            blk, s = divmod(n, 4)
            if s == 0 and blk + 1 < NB:
                load_mix(blk + 1, mixT[(blk + 1) % 2], tmx[(blk + 1) % 2])
            xT_, txT_ = x1T[blk % 2], tx1T[blk % 2]
            t0 = blk * 512 + s * 128
            xr, txr_ = xres[n % 2], txr[n % 2]
            r_, tr_ = rt[n % 2], trt[n % 2]
            x1_, tx1_ = x1[n % 2], tx1[n % 2]
            p_, tp_ = po[n % 2], tpo[n % 2]
            q_, tq_ = ptr[n % 2], tptr[n % 2]
            sc.dma("sp", xr[:], x_d[t0:t0 + 128, :], w=[txr_])
            if n + 1 < NTILE:
                mm(n + 1)
            for hf in range(2):
                hs = slice(hf * 512, (hf + 1) * 512)
                sc.c("dve", lambda e: e.scalar_tensor_tensor(r_[:, hs], xr[:, hs], ALPHA, p_[:, hs], ALU.mult, ALU.add), r=[txr_, tp_], w=[tr_])
            _layernorm(sc, (st, junk, tst), r_, tr_, g, b, tgb, x1_, tx1_)
            sc.dma("sp", x1_d[t0:t0 + 128, :], x1_[:], r=[tx1_])
            sc.c("act", lambda e: e.activation(x1b[:], x1_[:], AF.Copy), r=[tx1_], w=[tx1b])
            for c in range(8):
                sc.c("pe", lambda e: e.transpose(q_[:, c, :], x1b[:, c * 128:(c + 1) * 128], ident[:]), r=[tx1b, tW], w=[tq_])
            sc.c("act", lambda e: e.activation(xT_[:, :, s * 128:(s + 1) * 128], q_[:], AF.Copy), r=[tq_], w=[txT_])
            if s == 3:
                sc.dma("sp", x1T_d.rearrange("(c p) t -> p c t", p=128)[:, :, blk * 512:(blk + 1) * 512], xT_[:], r=[txT_])
        sc.barrier()


def phase_ffn(nc, sc, NT, x1_d, x1T_d, wg_d, wu_d, wd_d, g_d, b_d, out_d, pfx="p4b", pre=None):
    NF = D_FF // 128
    with ExitStack() as es:
        cx = Ctx(nc, es, pfx)
        tW = T()
        if pre is None:
            Wg = cx.sb([128, 8, D_FF], BF16, "Wg")
            Wu = cx.sb([128, 8, D_FF], BF16, "Wu")
            for (Wt, wd_) in ((Wg, wg_d), (Wu, wu_d)):
                wv = wd_.rearrange("(c p) n -> p c n", p=128)
                for c in range(8):
                    sc.dma("pool", Wt[:, c, :], wv[:, c, :], w=[tW])
        else:
            Wg, Wu = pre
        Wd = cx.sb([128, NF, 1024], BF16, "Wd")
        wv = wd_d.rearrange("(c p) n -> p c n", p=128)
        for c in range(NF):
            sc.dma("pool", Wd[:, c, :], wv[:, c, :], w=[tW])
        g = cx.sb([128, 1024], F32, "g")
        b = cx.sb([128, 1024], F32, "b")
        tgb = T()
        sc.dma("sp", g[:], g_d[0:1, :].partition_broadcast(128), w=[tgb])
        sc.dma("sp", b[:], b_d[0:1, :].partition_broadcast(128), w=[tgb])
        st = cx.sb([128, 8], F32, "st")
        junk = cx.sb([128, 1024], BF16, "junk")
        tst = T()
        sc.c("dve", lambda e: e.memset(st[:, 5:6], LN_EPS), w=[tst])
        x1T = [cx.sb([128, 8, 512], BF16, "x1T")] * 2
        tx1T = [T()] * 2
        hid = cx.sb([128, NF, 512], BF16, "hid")
        thid = T()
        sg = [cx.sb([128, 512], F32, "sg") for _ in range(2)]
        tsg = [T(), T()]
        x1 = [cx.sb([128, 1024], F32, "x1")] * 2
        tx1 = [T()] * 2
        rt = [cx.sb([128, 1024], F32, "rt") for _ in range(1)] * 2
        trt = [T()] * 2
        ot = [cx.sb([128, 1024], F32, "ot") for _ in range(1)] * 2
        tot = [T()] * 2
        pg = [cx.ps([128, 512], F32, "pg") for _ in range(2)]
        pu = [cx.ps([128, 512], F32, "pu") for _ in range(2)]
        tpg = [T(), T()]
        tpu = [T(), T()]
        po = [cx.ps([128, 1024], F32, "po") for _ in range(2)]
        tpo = [T(), T()]
        NB = NT // 512
        xv = x1T_d.rearrange("(c p) t -> p c t", p=128)
        n = 0
        nf = 0
        for blk in range(NB):
            xT_, txT_ = x1T[blk % 2], tx1T[blk % 2]
            sc.dma("sp", xT_[:], xv[:, :, blk * 512:(blk + 1) * 512], w=[txT_])
            for f in range(NF):
                pg_, tpg_ = pg[nf % 2], tpg[nf % 2]
                pu_, tpu_ = pu[nf % 2], tpu[nf % 2]
                sg_, tsg_ = sg[nf % 2], tsg[nf % 2]
                nf += 1
                for c in range(8):
                    sc.c("pe", lambda e: e.matmul(pg_[:], Wg[:, c, f * 128:(f + 1) * 128], xT_[:, c, :], start=(c == 0), stop=(c == 7)), r=[tW, txT_], w=[tpg_])
                for c in range(8):
                    sc.c("pe", lambda e: e.matmul(pu_[:], Wu[:, c, f * 128:(f + 1) * 128], xT_[:, c, :], start=(c == 0), stop=(c == 7)), r=[tW, txT_], w=[tpu_])
                sc.c("act", lambda e: e.activation(sg_[:], pg_[:], AF.Silu), r=[tpg_], w=[tsg_])
                sc.c("dve", lambda e: e.tensor_tensor(hid[:, f, :], sg_[:], pu_[:], ALU.mult), r=[tsg_, tpu_], w=[thid])
            for s in range(4):
                t0 = blk * 512 + s * 128
                x1_, tx1_ = x1[n % 2], tx1[n % 2]
                r_, tr_ = rt[n % 2], trt[n % 2]
                o_, to_ = ot[n % 2], tot[n % 2]
                p_, tp_ = po[n % 2], tpo[n % 2]
                n += 1
                sc.dma("sp", x1_[:], x1_d[t0:t0 + 128, :], w=[tx1_])
                for hf in range(2):
                    for f in range(NF):
                        sc.c("pe", lambda e: e.matmul(p_[:, hf * 512:(hf + 1) * 512], hid[:, f, s * 128:(s + 1) * 128], Wd[:, f, hf * 512:(hf + 1) * 512],
                                                     start=(f == 0), stop=(f == NF - 1)), r=[thid, tW], w=[tp_])
                for hf in range(2):
                    hs = slice(hf * 512, (hf + 1) * 512)
                    sc.c("dve", lambda e: e.scalar_tensor_tensor(r_[:, hs], x1_[:, hs], ALPHA, p_[:, hs], ALU.mult, ALU.add), r=[tx1_, tp_], w=[tr_])
                _layernorm(sc, (st, junk, tst), r_, tr_, g, b, tgb, o_, to_)
                sc.dma("pool", out_d[t0:t0 + 128, :], o_[:], r=[to_])
        sc.barrier()


from concourse.bass_utils import run_bass_kernel_spmd
import math

I32 = mybir.dt.int32
SEQ = 8192
BATCH = 4
NCORES = 8
HALF = SEQ // 2
DEPTH = 2
PAIRS = [[0, 1], [2, 3], [4, 5], [6, 7]]
HEADS_P = [[7, 5, 3, 1], [6, 4, 2, 0]]
WIN_SLOPES = [2.0 ** -(h + 1) for h in HEADS_P[0]]


def _const_tables(S, heads):
    pos = np.arange(S)
    r = (pos % 128).astype(np.float32)
    a = (pos // 128).astype(np.float32)
    kt = np.zeros((4, 8, S), np.float32)
    qt = np.zeros((4, 4, S), np.float32)
    dt = np.zeros((4, 128, 128), np.float32)
    kk = np.arange(128)[:, None]
    qq = np.arange(128)[None, :]
    for i, h in enumerate(heads):
        m = 2.0 ** (-(h + 1))
        kt[i, 0] = 1
        kt[i, 1] = 1
        kt[i, 2] = m * r
        kt[i, 3] = m * 128 * a
        kt[i, 4:8] = -kt[i, 0:4]
        qt[i, 0] = -m * r
        qt[i, 1] = -m * 128 * a
        qt[i, 2] = 1
        qt[i, 3] = 1
        dt[i] = -m * np.abs(qq - kk)
    return kt, qt, dt


def _tri_tables():
    k = np.arange(128)[:, None]
    j = np.arange(128)[None, :]
    return np.stack([(k <= j), (k < j), (k >= j), (k > j)]).astype(np.float32)


def build_fused(S=SEQ, depth=DEPTH):
    NT = S // 2
    nc = bass.Bass("TRN2", target_bir_lowering=False)
    I = lambda n, sh: nc.dram_tensor(n, sh, F32, kind="ExternalInput").ap()
    x_full = I("x_full", [S, 1024])
    x_own = I("x_own", [NT, 1024])
    pid = nc.dram_tensor("pid", [1, 1], I32, kind="ExternalInput").ap()
    ident = I("ident", [128, 128])
    ktab = I("ktab", [4, 8, S])
    qtab = I("qtab", [4, 4, S])
    dtab = I("dtab", [4, 128, 128])
    tri = I("tri", [4, 128, 128])
    L = []
    for l in range(depth):
        L.append(dict(
            w=I(f"w{l}", [1024, NCOL]), lam=I(f"lam{l}", [4, 64]), subln=I(f"subln{l}", [128, 1]), lin=I(f"lin{l}", [1, 2]),
            convw=I(f"convw{l}", [8, 128, 5]), convb=I(f"convb{l}", [8, 128, 1]), dtb=I(f"dtb{l}", [2, 8]),
            alog=I(f"alog{l}", [2, 8]), dsk=I(f"dsk{l}", [1, 8]), normw=I(f"normw{l}", [1, 512]),
            wo=I(f"wo{l}", [2048, 1024]), g1=I(f"g1{l}", [1, 1024]), b1=I(f"b1{l}", [1, 1024]),
            g2=I(f"g2{l}", [1, 1024]), b2=I(f"b2{l}", [1, 1024]),
            wg=I(f"wg{l}", [1024, D_FF]), wu=I(f"wu{l}", [1024, D_FF]), wd=I(f"wd{l}", [D_FF, 1024])))
    out = nc.dram_tensor("out", [NT, 1024], F32, kind="ExternalOutput").ap()
    fm = nc.dram_tensor("fm", [16, 128, S], BF16).ap()
    v = nc.dram_tensor("v", [S, 512], BF16).ap()
    z = nc.dram_tensor("z", [S, 512], BF16).ap()
    dt = nc.dram_tensor("dt", [S, 8], F32).ap()
    rows = nc.dram_tensor("rows", [2, S // 128, 512], F32).ap()
    mix = nc.dram_tensor("mix", [1024, S], BF16)
    gath = [nc.dram_tensor(f"gath{l}", [8, 256, S], BF16) for l in range(depth)]
    x1 = nc.dram_tensor("x1", [NT, 1024], F32).ap()
    x1T = nc.dram_tensor("x1T", [1024, NT], BF16).ap()
    mixmine = nc.dram_tensor("mixmine", [2048, NT], BF16)
    xo = [nc.dram_tensor(f"xo{l}", [NT, 1024], F32) for l in range(depth - 1)]
    NXC = NT // 512
    xg = [nc.dram_tensor(f"xg{l}", [NXC, 1024, 1024], F32) for l in range(depth - 1)]
    sc = Sched(nc)
    pt = nc.alloc_sbuf_tensor("pid_t", [1, 1], I32)
    tpid = T()
    sc.dma("pool", pt[:], pid, w=[tpid])
    sc._wait("pool", tpid.w)
    reg_p = nc.gpsimd.alloc_register("reg_p")
    reg_t = nc.gpsimd.alloc_register("reg_t")
    nc.gpsimd.reg_load(reg_p, pt[:1, :1])
    nc.gpsimd.reg_mul(reg_p, reg_p, NT)
    for l in range(depth):
        P = L[l]
        if l == 0:
            xin = x_full
        else:
            xin = (lambda j, h=xg[l - 1]: h.ap()[j % NXC, (j // NXC) * 512:(j // NXC + 1) * 512, :])
        xres = x_own if l == 0 else xo[l - 1].ap()
        phase_inproj(nc, sc, S, xin, P["w"], ident, fm, v, z, dt, pfx=f"p1_{l}")
        phase_attn(nc, sc, S, fm, v, ktab, qtab, dtab, ident, P["lam"], P["subln"], P["lin"], mix.ap(), pfx=f"p2_{l}",
                   win_slopes=WIN_SLOPES)
        phase_ssd(nc, sc, S, fm, z, dt, P["convw"], P["convb"], P["dtb"], P["alog"], P["dsk"], P["normw"], tri, ident,
                  rows, mix.ap(), pfx=f"p3_{l}")
        sc.collective("AllGather", [(mix.ap()[k * 128:(k + 1) * 128, :], gath[l].ap()[k]) for k in range(8)], PAIRS)
        tmm = T()
        for gi, (rank, rb) in enumerate(((0, 0), (1, 0), (0, 512), (1, 512))):
            nc.gpsimd.reg_add(reg_t, reg_p, ((rb // 128) * 2 + rank) * 128 * S)
            sc.dma("pool", mixmine.ap()[gi * 512:(gi + 1) * 512, :].rearrange("(k r) t -> k r t", r=128),
                   bass.AP(gath[l], reg_t, [[2 * 128 * S, 4], [S, 128], [1, NT]]), w=[tmm])
        sc.barrier()
        dst = out if l == depth - 1 else xo[l].ap()
        with ExitStack() as es4:
            Wg_ = es4.enter_context(nc.sbuf_tensor(f"pre_Wg{l}", [128, 8, D_FF], BF16))
            Wu_ = es4.enter_context(nc.sbuf_tensor(f"pre_Wu{l}", [128, 8, D_FF], BF16))
            phase_outproj(nc, sc, NT, xres, mixmine.ap(), P["wo"], P["g1"], P["b1"], ident, x1, x1T, pfx=f"p4a_{l}",
                          prefetch=((Wg_, P["wg"]), (Wu_, P["wu"])))
            phase_ffn(nc, sc, NT, x1, x1T, P["wg"], P["wu"], P["wd"], P["g2"], P["b2"], dst, pfx=f"p4b_{l}", pre=(Wg_, Wu_))
        if l < depth - 1:
            sc.collective("AllGather", [(xo[l].ap()[k * 512:(k + 1) * 512, :], xg[l].ap()[k]) for k in range(NXC)], PAIRS)
    return nc


def _c(a):
    return np.ascontiguousarray(a)


def make_in_maps(x, w_in, lambda_q1, lambda_k1, lambda_q2, lambda_k2, subln_w, conv_w, conv_b,
                 dt_bias_fwd, dt_bias_bwd, a_log_fwd, a_log_bwd, d_skip, ssm_norm_w, w_out,
                 ln1_g, ln1_b, w_gate, w_up, w_down, ln2_g, ln2_b, S=SEQ, ncores=NCORES):
    f32 = np.float32
    A = lambda t: np.asarray(t, f32)
    x = A(x)
    NT = S // 2
    ident = np.eye(128, dtype=f32)
    tri = _tri_tables()
    tabs = [_const_tables(S, HEADS_P[p]) for p in range(2)]
    depth = np.asarray(w_in).shape[0]
    worows = np.concatenate([np.arange(h * 128, (h + 1) * 128) for h in HEADS_P[0] + HEADS_P[1]] + [np.arange(1024, 2048)])
    in_maps = []
    for core in range(ncores):
        b, p = core // 2, core % 2
        hcols = np.concatenate([np.arange(h * 128, (h + 1) * 128) for h in HEADS_P[p]])
        cols = np.concatenate([
            hcols,
            1024 + hcols,
            4096 + np.arange(512 * p, 512 * p + 512),
            4096 + 1024 + np.arange(256 * p, 256 * p + 256),
            4096 + 1536 + np.arange(256 * p, 256 * p + 256),
            2048 + hcols,
            3072 + np.arange(512 * p, 512 * p + 512),
            6144 + np.arange(8 * p, 8 * p + 8),
        ])
        cidx = np.concatenate([
            np.arange(512 * p, 512 * p + 512),
            1024 + np.arange(256 * p, 256 * p + 256),
            1536 + np.arange(256 * p, 256 * p + 256),
        ])
        kt, qt, dtb_ = tabs[p]
        hs = slice(8 * p, 8 * p + 8)
        m = dict(x_full=_c(x[b]), x_own=_c(x[b, p * NT:(p + 1) * NT]), pid=np.array([[p]], np.int32),
                 ident=ident, ktab=kt, qtab=qt, dtab=dtb_, tri=tri)
        for l in range(depth):
            lambda_init = 0.8 - 0.6 * math.exp(-0.3 * l)
            m[f"w{l}"] = _c(A(w_in[l])[:, cols])
            m[f"lam{l}"] = _c(np.stack([A(lambda_q1[l]), A(lambda_k1[l]), A(lambda_q2[l]), A(lambda_k2[l])]))
            m[f"subln{l}"] = _c(A(subln_w[l]).reshape(128, 1))
            m[f"lin{l}"] = np.array([[-lambda_init, 1.0 - lambda_init]], f32)
            m[f"convw{l}"] = _c(A(conv_w[l])[:, cidx].T.reshape(8, 128, 5))
            m[f"convb{l}"] = _c(A(conv_b[l])[cidx].reshape(8, 128, 1))
            m[f"dtb{l}"] = _c(np.stack([A(dt_bias_fwd[l])[hs], A(dt_bias_bwd[l])[hs]]))
            m[f"alog{l}"] = _c(np.stack([A(a_log_fwd[l])[hs], A(a_log_bwd[l])[hs]]))
            m[f"dsk{l}"] = _c(A(d_skip[l])[hs].reshape(1, 8))
            m[f"normw{l}"] = _c(A(ssm_norm_w[l])[512 * p:512 * p + 512].reshape(1, 512))
            m[f"wo{l}"] = _c(A(w_out[l])[worows])
            m[f"g1{l}"] = _c(A(ln1_g[l]).reshape(1, 1024))
            m[f"b1{l}"] = _c(A(ln1_b[l]).reshape(1, 1024))
            m[f"g2{l}"] = _c(A(ln2_g[l]).reshape(1, 1024))
            m[f"b2{l}"] = _c(A(ln2_b[l]).reshape(1, 1024))
            m[f"wg{l}"] = _c(A(w_gate[l]))
            m[f"wu{l}"] = _c(A(w_up[l]))
            m[f"wd{l}"] = _c(A(w_down[l]))
        in_maps.append(m)
    return in_maps


def kernel(**inputs):
    in_maps = make_in_maps(**inputs)
    nc = build_fused()
    res = run_bass_kernel_spmd(nc, in_maps, core_ids=list(range(NCORES)))
    outp = np.empty((BATCH, SEQ, 1024), np.float32)
    for core in range(NCORES):
        b, p = core // 2, core % 2
        outp[b, p * HALF:(p + 1) * HALF] = np.asarray(res.results[core]["out"], np.float32)
    return outp
```
